# Optimizing a Trainium2 kernel written in Bass

```python
import jax
import jax.numpy as jnp
from jax import lax
import numpy as np


D_MODEL = 1024
BATCH = 4
SEQ = 4096
DEPTH = 4
DEC_BATCH = 4
DEC_SEQ = 8192
PAST_LEN = 128

GRID_W = 64
PLE_DIM = 256
EPS = 1e-6
POOL_WIDTH = 512
POOL_GROUPS = 4
POOL_GDIM = POOL_WIDTH // POOL_GROUPS
POOL_WINDOWS = (2, 4, 8, 16)
HG_HEADS = 4
HG_DK = 128
HG_DV = 128
HG_KW = HG_HEADS * HG_DK
HG_VW = HG_HEADS * HG_DV
HG_CHUNK = 64
ATT_HEADS = 16
ATT_KV_HEADS = 4
HEAD_DIM = 64
ATT_GROUP = ATT_HEADS // ATT_KV_HEADS
ATT_QW = ATT_HEADS * HEAD_DIM
ATT_KVW = ATT_KV_HEADS * HEAD_DIM
Q_BLOCK = 128
ROPE_THETA = 10000.0
ROPE_HALF = HEAD_DIM // 2
N_BRANCH = 3
D_FF = 2816
CONV_W = 3

OFF_POOL = 0
OFF_HQ = OFF_POOL + POOL_WIDTH
OFF_HFF = OFF_HQ + HG_KW
OFF_HFB = OFF_HFF + HG_KW
OFF_HI = OFF_HFB + HG_KW
OFF_HG = OFF_HI + HG_VW
OFF_AQ = OFF_HG + HG_VW
OFF_AK = OFF_AQ + ATT_QW
OFF_AV = OFF_AK + ATT_KVW
OFF_GATE = OFF_AV + ATT_KVW
IN_WIDTH = OFF_GATE + N_BRANCH * D_MODEL

kernel_name = 'hybrid_pool_hgrn2_gqa_encoder'


def _rmsnorm(x, g):
    xf = x.astype(jnp.float32)
    y = xf * lax.rsqrt(jnp.mean(xf * xf, axis=-1, keepdims=True) + EPS)
    return (y * g.astype(jnp.float32)).astype(x.dtype)


def _pool_mixer(u, w_grp, scale):
    b, n, _ = u.shape
    ug = u.reshape(b, n, POOL_GROUPS, POOL_GDIM).astype(jnp.float32)
    cs = jnp.concatenate([jnp.zeros((b, 1, POOL_GROUPS, POOL_GDIM), jnp.float32), jnp.cumsum(ug, axis=1)], axis=1)
    t = jnp.arange(n)[:, None]
    half = jnp.array([w // 2 for w in POOL_WINDOWS], jnp.int32)[None, :]
    lo = jnp.clip(t - half, 0, n)
    hi = jnp.clip(t + half, 0, n)
    gi = jnp.arange(POOL_GROUPS)[None, :]
    win_sum = cs[:, hi, gi, :] - cs[:, lo, gi, :]
    cnt = (hi - lo).astype(jnp.float32)[None, :, :, None]
    mixed = (win_sum / cnt - ug).astype(u.dtype)
    y = jnp.einsum('bngc,gcd->bngd', mixed, w_grp) * scale.reshape(POOL_GROUPS, POOL_GDIM)
    return y.reshape(b, n, POOL_WIDTH)


def _lower_bound(lb_raw):
    s = jnp.cumsum(jax.nn.softmax(lb_raw.astype(jnp.float32), axis=0), axis=0)
    return s - s[0:1]


def _gla_chunk_scan(q, k, v, logf):
    b, n, h, dk = q.shape
    dv = v.shape[-1]
    nc = n // HG_CHUNK

    def chunks(a):
        return a.reshape(b, nc, HG_CHUNK, h, a.shape[-1]).transpose(1, 0, 3, 2, 4)

    causal = jnp.tril(jnp.ones((HG_CHUNK, HG_CHUNK), bool))[:, :, None]

    def step(state, xs):
        qc, kc, vc, gc = xs
        cum = jnp.cumsum(gc, axis=2)
        rel = jnp.where(causal, cum[:, :, :, None, :] - cum[:, :, None, :, :], -jnp.inf)
        scores = jnp.einsum('bhjc,bhjlc,bhlc->bhjl', qc, jnp.exp(rel), kc)
        out = jnp.einsum('bhjl,bhlv->bhjv', scores, vc) + jnp.einsum('bhjc,bhcv->bhjv', qc * jnp.exp(cum), state)
        last = cum[:, :, -1:, :]
        state = jnp.exp(last[:, :, 0, :])[..., None] * state + jnp.einsum('bhlc,bhlv->bhcv', kc * jnp.exp(last - cum), vc)
        return state, out

    s0 = jnp.zeros((b, h, dk, dv), jnp.float32)
    _, out = lax.scan(step, s0, (chunks(q), chunks(k), chunks(v), chunks(logf)))
    return out.transpose(1, 0, 3, 2, 4).reshape(b, n, h, dv)


def _hgrn2_mixer(zq, zff, zfb, zi, zg, lb_f, lb_b, g_onorm):
    b, n, _ = zq.shape

    def heads(a, d):
        return a.astype(jnp.float32).reshape(b, n, HG_HEADS, d)

    q = heads(zq, HG_DK) * HG_DK ** -0.5
    v = heads(zi, HG_DV)

    def gates(z, lb):
        lbh = lb.reshape(HG_HEADS, HG_DK)
        logf = jnp.logaddexp(jnp.log(lbh), jnp.log1p(-lbh) + jax.nn.log_sigmoid(z))
        k = (1.0 - lbh) * jax.nn.sigmoid(-z)
        return k, logf

    k_fw, logf_fw = gates(heads(zff, HG_DK), lb_f)
    k_bw, logf_bw = gates(heads(zfb, HG_DK), lb_b)
    o_fw = _gla_chunk_scan(q, k_fw, v, logf_fw)

    def flip(a):
        return jnp.flip(a, axis=1)

    o_bw = flip(_gla_chunk_scan(flip(q), flip(k_bw), flip(v), flip(logf_bw)))
    o = _rmsnorm(o_fw + o_bw, g_onorm).reshape(b, n, HG_VW) * jax.nn.silu(zg.astype(jnp.float32))
    return o.astype(zq.dtype)


def _axial_rope_tables(n):
    rows = n // GRID_W
    row = jnp.repeat(jnp.arange(rows, dtype=jnp.float32), GRID_W)
    col = jnp.tile(jnp.arange(GRID_W, dtype=jnp.float32), rows)
    inv_freq = 1.0 / (ROPE_THETA ** (jnp.arange(0, ROPE_HALF, 2, dtype=jnp.float32) / ROPE_HALF))
    ang = jnp.concatenate([row[:, None] * inv_freq, col[:, None] * inv_freq], axis=-1)
    return jnp.cos(ang), jnp.sin(ang)


def _apply_axial_rope(x, cos, sin):
    b, n, h, d = x.shape
    xs = x.reshape(b, n, h, 2, 2, ROPE_HALF // 2)
    c = cos.reshape(n, 1, 2, ROPE_HALF // 2)
    s = sin.reshape(n, 1, 2, ROPE_HALF // 2)
    x1 = xs[..., 0, :]
    x2 = xs[..., 1, :]
    y1 = x1 * c - x2 * s
    y2 = x2 * c + x1 * s
    return jnp.stack([y1, y2], axis=-2).reshape(b, n, h, d)


def _gqa_axial(zq, zk, zv, g_q, g_k, cos, sin):
    b, n, _ = zq.shape
    dt = zq.dtype
    q = _rmsnorm(zq.reshape(b, n, ATT_HEADS, HEAD_DIM), g_q).astype(jnp.float32)
    k = _rmsnorm(zk.reshape(b, n, ATT_KV_HEADS, HEAD_DIM), g_k).astype(jnp.float32)
    q = (_apply_axial_rope(q, cos, sin) * HEAD_DIM ** -0.5).astype(dt)
    k = _apply_axial_rope(k, cos, sin).astype(dt)
    v = zv.reshape(b, n, ATT_KV_HEADS, HEAD_DIM)
    nb = n // Q_BLOCK
    qb = q.reshape(b, nb, Q_BLOCK, ATT_KV_HEADS, ATT_GROUP, HEAD_DIM).transpose(1, 0, 3, 4, 2, 5)
    kt = k.transpose(0, 2, 1, 3)
    vt = v.transpose(0, 2, 1, 3)

    def block(qblk):
        s = jnp.einsum('bkgqd,bknd->bkgqn', qblk, kt, preferred_element_type=jnp.float32)
        p = jax.nn.softmax(s, axis=-1)
        return jnp.einsum('bkgqn,bknd->bkgqd', p.astype(dt), vt)

    o = lax.map(block, qb)
    return o.transpose(1, 0, 4, 2, 3, 5).reshape(b, n, ATT_QW)


def _conv_glu_ffn(h, w_up, conv_w, conv_b, w_down):
    u = h @ w_up
    up = jnp.pad(u, ((0, 0), (1, 1), (0, 0)))
    c = up[:, :-2] * conv_w[0] + up[:, 1:-1] * conv_w[1] + up[:, 2:] * conv_w[2] + conv_b
    a, g = jnp.split(c, 2, axis=-1)
    return (jax.nn.gelu(a, approximate=True) * g) @ w_down


def _trunk(x, p, cos, sin, g_mix, w_in, pool_w, pool_scale, lb_f, lb_b, hg_onorm, g_q, g_k,
           w_br_pool, w_br_hg, w_br_att, w_out, g_ffn, w_up, conv_w, conv_b, w_down,
           g_ple, w_ple_gate, w_ple, g_final):
    for i in range(DEPTH):
        h = _rmsnorm(x, g_mix[i])
        z = h @ w_in[i]
        y_pool = _pool_mixer(z[..., OFF_POOL:OFF_HQ], pool_w[i], pool_scale[i])
        y_hg = _hgrn2_mixer(z[..., OFF_HQ:OFF_HFF], z[..., OFF_HFF:OFF_HFB], z[..., OFF_HFB:OFF_HI],
                            z[..., OFF_HI:OFF_HG], z[..., OFF_HG:OFF_AQ], lb_f[i], lb_b[i], hg_onorm[i])
        y_att = _gqa_axial(z[..., OFF_AQ:OFF_AK], z[..., OFF_AK:OFF_AV], z[..., OFF_AV:OFF_GATE],
                           g_q[i], g_k[i], cos, sin)
        gates = jax.nn.sigmoid(z[..., OFF_GATE:].astype(jnp.float32)).astype(x.dtype)
        gates = gates.reshape(z.shape[0], z.shape[1], N_BRANCH, D_MODEL)
        merged = (gates[..., 0, :] * (y_pool @ w_br_pool[i])
                  + gates[..., 1, :] * (y_hg @ w_br_hg[i])
                  + gates[..., 2, :] * (y_att @ w_br_att[i]))
        x = x + merged @ w_out[i]
        x = x + _conv_glu_ffn(_rmsnorm(x, g_ffn[i]), w_up[i], conv_w[i], conv_b[i], w_down[i])
        ple_gate = jax.nn.sigmoid(_rmsnorm(x, g_ple[i]) @ w_ple_gate[i])
        x = x + ple_gate * (p[i] @ w_ple[i])
    return _rmsnorm(x, g_final)


def setup_inputs(seed: int = 0) -> dict:
    key = jax.random.key(seed)
    ks = iter(jax.random.split(key, 32))

    def nrm(shape, scale):
        return jax.random.normal(next(ks), shape, jnp.float32) * scale

    def gain(shape):
        return 1.0 + nrm(shape, 0.02)

    return {
        'x_prompt': nrm((BATCH, SEQ, D_MODEL), 1.0),
        'x_sample': nrm((DEC_BATCH, DEC_SEQ, D_MODEL), 1.0),
        'p_prompt': nrm((DEPTH, BATCH, SEQ, PLE_DIM), 1.0),
        'p_sample': nrm((DEPTH, DEC_BATCH, DEC_SEQ, PLE_DIM), 1.0),
        'g_mix': gain((DEPTH, D_MODEL)),
        'w_in': nrm((DEPTH, D_MODEL, IN_WIDTH), D_MODEL ** -0.5),
        'pool_w': nrm((DEPTH, POOL_GROUPS, POOL_GDIM, POOL_GDIM), POOL_GDIM ** -0.5),
        'pool_scale': 1.0 + nrm((DEPTH, POOL_WIDTH), 0.1),
        'lb_raw_f': nrm((DEPTH, HG_KW), 0.1),
        'lb_raw_b': nrm((DEPTH, HG_KW), 0.1),
        'hg_onorm': gain((DEPTH, HG_DV)),
        'g_q': gain((DEPTH, HEAD_DIM)),
        'g_k': gain((DEPTH, HEAD_DIM)),
        'w_br_pool': nrm((DEPTH, POOL_WIDTH, D_MODEL), POOL_WIDTH ** -0.5),
        'w_br_hg': nrm((DEPTH, HG_VW, D_MODEL), HG_VW ** -0.5),
        'w_br_att': nrm((DEPTH, ATT_QW, D_MODEL), ATT_QW ** -0.5),
        'w_out': nrm((DEPTH, D_MODEL, D_MODEL), D_MODEL ** -0.5),
        'g_ffn': gain((DEPTH, D_MODEL)),
        'w_up': nrm((DEPTH, D_MODEL, 2 * D_FF), D_MODEL ** -0.5),
        'conv_w': nrm((DEPTH, CONV_W, 2 * D_FF), CONV_W ** -0.5),
        'conv_b': nrm((DEPTH, 2 * D_FF), 0.01),
        'w_down': nrm((DEPTH, D_FF, D_MODEL), D_FF ** -0.5),
        'g_ple': gain((DEPTH, D_MODEL)),
        'w_ple_gate': nrm((DEPTH, D_MODEL, D_MODEL), D_MODEL ** -0.5),
        'w_ple': nrm((DEPTH, PLE_DIM, D_MODEL), PLE_DIM ** -0.5),
        'g_final': gain((D_MODEL,)),
    }


def reference(x_prompt, x_sample, p_prompt, p_sample, g_mix, w_in, pool_w, pool_scale, lb_raw_f, lb_raw_b,
              hg_onorm, g_q, g_k, w_br_pool, w_br_hg, w_br_att, w_out, g_ffn, w_up, conv_w, conv_b, w_down,
              g_ple, w_ple_gate, w_ple, g_final):
    lb_f = _lower_bound(lb_raw_f)
    lb_b = _lower_bound(lb_raw_b)
    weights = (g_mix, w_in, pool_w, pool_scale, lb_f, lb_b, hg_onorm, g_q, g_k, w_br_pool, w_br_hg, w_br_att,
               w_out, g_ffn, w_up, conv_w, conv_b, w_down, g_ple, w_ple_gate, w_ple, g_final)
    cos_p, sin_p = _axial_rope_tables(x_prompt.shape[1])
    cos_s, sin_s = _axial_rope_tables(x_sample.shape[1])
    y_prompt = _trunk(x_prompt, p_prompt, cos_p, sin_p, *weights)
    y_sample = _trunk(x_sample, p_sample, cos_s, sin_s, *weights)
    return (y_prompt, y_sample)
```

```python
import numpy as np
from contextlib import ExitStack
import concourse.bass as bass
import concourse.mybir as mybir
from concourse.bass_utils import run_bass_kernel_spmd

F32, BF16 = mybir.dt.float32, mybir.dt.bfloat16
AF = mybir.ActivationFunctionType
ALU = mybir.AluOpType

D = 1024
EPS = 1e-6
DFF = 2816
NFC = 2 * DFF // 128
PL = 1 + 1 + 1 + 4 + 3 * NFC + NFC
SAME_ENGINE_SYNC = True


class Op:
    __slots__ = ("eng", "fn", "deps", "isdma", "needed", "tok", "back")

    def __init__(self, eng, fn, isdma):
        self.eng, self.fn, self.isdma = eng, fn, isdma
        self.deps = set()
        self.needed = False
        self.tok = None
        self.back = None


class Prog:
    K = 8

    def __init__(self, nc):
        self.nc = nc
        self.ops = []
        self.last_w = {}
        self.readers = {}
        self.last_op = {}
        self.dma_recent = {"sp": [], "pool": [], "act": []}

    def op(self, eng, fn, r=(), w=(), dma=False):
        o = Op(eng, fn, dma)
        for x in r:
            lw = self.last_w.get(x)
            if lw is not None:
                o.deps.add(lw)
        for x in w:
            lw = self.last_w.get(x)
            if lw is not None:
                o.deps.add(lw)
            for rd in self.readers.get(x, ()):
                o.deps.add(rd)
        for x in r:
            self.readers.setdefault(x, []).append(o)
        for x in w:
            self.last_w[x] = o
            self.readers[x] = []
        o.deps.discard(o)
        self.ops.append(o)
        if dma:
            lst = self.dma_recent[eng]
            lst.append(o)
            if len(lst) > self.K:
                lst.pop(0)
        else:
            self.last_op[eng] = o
        return o

    def barrier(self):
        deps = set(self.last_op.values())
        for lst in self.dma_recent.values():
            deps.update(lst)
        for eng in ("pe", "act", "dve", "pool", "sp"):
            o = Op(eng, None, False)
            o.deps = set(deps)
            self.ops.append(o)
        self.last_w.clear()
        self.readers.clear()

    def emit(self):
        nc = self.nc
        for o in self.ops:
            for d in o.deps:
                d.needed = True
        cnt = {}
        dcnt = {}
        for o in self.ops:
            if o.isdma:
                k = dcnt.get(o.eng, 0)
                o.tok = (("d", o.eng, k % self.K), 16 * (k // self.K + 1))
                o.back = (("d", o.eng, k % self.K), 16 * (k // self.K)) if k >= self.K else None
                dcnt[o.eng] = k + 1
            elif o.fn is not None and o.needed:
                cnt[o.eng] = cnt.get(o.eng, 0) + 1
                o.tok = (("c", o.eng), cnt[o.eng])
        handles = {"pe": nc.tensor, "act": nc.scalar, "dve": nc.vector, "pool": nc.gpsimd, "sp": nc.sync}
        with ExitStack() as es:
            sems = {}
            for e in ("pe", "act", "dve", "pool"):
                sems[("c", e)] = es.enter_context(nc.semaphore("c_" + e))
            for e in dcnt:
                for k in range(self.K):
                    sems[("d", e, k)] = es.enter_context(nc.semaphore("d_%s%d" % (e, k)))
            block = es.enter_context(nc.Block())
            by_eng = {e: [o for o in self.ops if o.eng == e] for e in handles}

            def run(ename, h):
                waited = {}
                for o in by_eng[ename]:
                    need = {}
                    for d in o.deps:
                        if d.tok is None:
                            continue
                        if d.eng == ename and not d.isdma and not o.isdma:
                            if ename == "pe" or not SAME_ENGINE_SYNC:
                                continue
                        s, v = d.tok
                        if need.get(s, 0) < v:
                            need[s] = v
                    if o.back is not None:
                        s, v = o.back
                        if need.get(s, 0) < v:
                            need[s] = v
                    for s, v in need.items():
                        if waited.get(s, 0) < v:
                            h.wait_ge(sems[s], v)
                            waited[s] = v
                    if o.fn is None:
                        continue
                    inst = o.fn(h)
                    if o.tok is not None:
                        inst.then_inc(sems[o.tok[0]], 16 if o.isdma else 1)

            @block.tensor
            def _(h):
                run("pe", h)

            @block.scalar
            def _(h):
                run("act", h)

            @block.vector
            def _(h):
                run("dve", h)

            @block.gpsimd
            def _(h):
                run("pool", h)

            @block.sync
            def _(h):
                run("sp", h)


class Rot:
    def __init__(self, tiles, name):
        self.tiles, self.name, self.i = tiles, name, 0

    def get(self):
        i = self.i % len(self.tiles)
        self.i += 1
        return self.tiles[i], "%s%d" % (self.name, i)


class Builder:
    def __init__(self, NS, DEPTH, dbg=False):
        self.NS, self.DEPTH, self.dbg = NS, DEPTH, dbg
        self.NT, self.NG, self.NCH = NS // 512, NS // 128, NS // 64
        self.nc = bass.Bass("TRN2", target_bir_lowering=False)
        self.p = Prog(self.nc)
        self.bk = 0

    def din(self, name, shape, dt=F32):
        return self.nc.dram_tensor(name, list(shape), dt, kind="ExternalInput").ap()

    def dsc(self, name, shape, dt):
        return self.nc.dram_tensor(name, list(shape), dt, kind="ExternalOutput" if self.dbg else "Internal").ap()

    def sb(self, es, name, shape, dt):
        self.uid = getattr(self, "uid", 0) + 1
        return es.enter_context(self.nc.sbuf_tensor("%s_u%d" % (name, self.uid), list(shape), dt))

    def rot(self, es, name, shape, dt, n):
        return Rot([self.sb(es, "%s_%d" % (name, i), shape, dt) for i in range(n)], name)

    def bank(self, lo=0, hi=8):
        n = hi - lo
        b = lo + (self.bk % n)
        self.bk += 1
        return b, "ps%d" % b

    def psb(self, b, w=512):
        return self.PS[:, b * 512:b * 512 + w]

    def mm(self, out, lhsT, rhs, start, stop, r, w):
        return self.p.op("pe", lambda e: e.matmul(out, lhsT, rhs, start=start, stop=stop), r, w)

    def tr(self, out, in_, ident, r, w):
        return self.p.op("pe", lambda e: e.transpose(out, in_, ident), r, w)

    def act(self, out, in_, func, r, w, bias=None, scale=None, accum=None):
        kw = {}
        if bias is not None:
            kw["bias"] = bias
        if scale is not None:
            kw["scale"] = scale
        if accum is not None:
            kw["accum_out"] = accum
        return self.p.op("act", lambda e: e.activation(out, in_, func, **kw), r, w)

    def tt(self, out, a, b, op, r, w, eng="dve"):
        return self.p.op(eng, lambda e: e.tensor_tensor(out, a, b, op), r, w)

    def ts(self, out, a, s1, s2, op0, op1, r, w, eng="dve"):
        return self.p.op(eng, lambda e: e.tensor_scalar(out, a, s1, s2, op0, op1), r, w)

    def stt(self, out, a, s, b, op0, op1, r, w, eng="dve"):
        return self.p.op(eng, lambda e: e.scalar_tensor_tensor(out, a, s, b, op0, op1), r, w)

    def cp(self, out, in_, r, w, eng="dve"):
        if eng == "act":
            return self.p.op("act", lambda e: e.copy(out, in_), r, w)
        return self.p.op(eng, lambda e: e.tensor_copy(out, in_), r, w)

    def ms(self, ap, val, w, eng="dve"):
        return self.p.op(eng, lambda e: e.memset(ap, val), (), w)

    def dma(self, out, in_, r, w, q="sp", **kw):
        return self.p.op(q, lambda e: e.dma_start(out=out, in_=in_, **kw), r, w, dma=True)

    def wload(self, dst, src3, key, nk):
        for k in range(nk):
            self.dma(dst[:, k, :], src3[:, k, :], (), [key], q="pool", max_dma_last_dim=8192)

    def ppc(self, col):
        return self.PP[:, col:col + 1]

    def lbase(self, L):
        return 2 * self.DEPTH * 4 + L * PL

    def norm_rows(self, x_ap, xkey, gbc, gkey, maskcol, h_ap, hkey):
        junk, jk = self.r_junk.get()
        ss, sk = self.r_ss.get()
        self.act(junk[:], x_ap, AF.Square, [xkey], [jk, sk], accum=ss[:, 0:1])
        self.act(ss[:, 1:2], ss[:, 0:1], AF.Ln, [sk], [sk], bias=self.epsc[:, 0:1], scale=1.0 / D)
        self.act(ss[:, 2:3], ss[:, 1:2], AF.Exp, [sk], [sk], scale=-0.5)
        rs = ss[:, 2:3]
        if maskcol is not None:
            self.tt(ss[:, 3:4], ss[:, 2:3], maskcol, ALU.mult, [sk, "mask"], [sk])
            rs = ss[:, 3:4]
        self.stt(h_ap, x_ap, rs, gbc, ALU.mult, ALU.mult, [xkey, sk, gkey], [hkey])

    def transpose_rows(self, h_ap, hkey, dst3, dkey, col0, evac="act"):
        b, bkey = self.bank()
        psv = self.psb(b).bitcast(BF16)
        for k in range(8):
            self.tr(psv[:, k * 128:(k + 1) * 128], h_ap[:, k * 128:(k + 1) * 128], self.identb[:], [hkey, "cst"], [bkey])
        self.cp(dst3[:, :, col0:col0 + 128], psv.rearrange("p (k n) -> p k n", k=8), [bkey], [dkey], eng=evac)

    def build(self):
        nc, NS, DEPTH = self.nc, self.NS, self.DEPTH
        NT, NG, NCH = self.NT, self.NG, self.NCH
        self.x = self.din("x", [NS, D])
        self.pT = self.din("pT", [DEPTH, 256, NS])
        self.maskc = self.din("maskc", [128, NG])
        self.rc = self.din("rc", [4, NS])
        self.cosT = self.din("cosT", [128, NS])
        self.sinT = self.din("sinT", [128, NS])
        self.cst = self.din("cst", [128, 1280])
        self.gv = self.din("gv", [3 * DEPTH + 1, D])
        self.ppd = self.din("pp", [128, 8 * DEPTH + DEPTH * PL])
        self.w_inA = self.din("w_inA", [DEPTH, D, 4864])
        self.w_inG = self.din("w_inG", [DEPTH, D, 3072])
        self.poolw = self.din("poolw", [DEPTH, 128, 4, 128])
        self.wbp = self.din("wbp", [DEPTH, 512, D])
        self.wbh = self.din("wbh", [DEPTH, 512, D])
        self.wba = self.din("wba", [DEPTH, D, D])
        self.wout = self.din("wout", [DEPTH, D, D])
        self.wup = self.din("wup", [DEPTH, D, 2 * DFF])
        self.wdn = self.din("wdn", [DEPTH, DFF, D])
        self.wpg = self.din("wpg", [DEPTH, D, D])
        self.wpl = self.din("wpl", [DEPTH, 256, D])
        self.y = self.nc.dram_tensor("y", [NS, D], F32, kind="ExternalOutput").ap()
        self.XS = self.dsc("XS", [NS, D], F32)
        self.H1T = self.dsc("H1T", [D, NS], BF16)
        self.UT = self.dsc("UT", [512, NS + 16], BF16)
        self.QK = {(d, k): self.dsc("QK%s%s" % (d, k), [512, NS], BF16) for d in "fb" for k in "qk"}
        self.SG = self.dsc("SG", [512, NS], BF16)
        self.VH = self.dsc("VH", [NS, 512], BF16)
        self.QT = self.dsc("QT", [D, NS], BF16)
        self.KT2 = self.dsc("KT2", [512, NS], BF16)
        self.VA = self.dsc("VA", [NS, 260], BF16)
        self.OO = {d: self.dsc("OO" + d, [512, NS], F32) for d in "fb"}
        self.YA = self.dsc("YA", [D, NS], BF16)
        self.H2T = self.dsc("H2T", [D, NS + 2], BF16)
        self.MG = self.dsc("MG", [D, NS], BF16)

        with ExitStack() as es:
            self.PS = es.enter_context(nc.psum_tensor("PS", [128, 4096], F32))
            self.CST = self.sb(es, "CST", [128, 1280], F32)
            self.PP = self.sb(es, "PP", [128, 8 * DEPTH + DEPTH * PL], F32)
            self.LB = self.sb(es, "LB", [128, 2, 2, DEPTH * 4], F32)
            self.identb = self.sb(es, "identb", [128, 128], BF16)
            self.BDb = self.sb(es, "BDb", [128, 128], BF16)
            self.ONESb = self.sb(es, "ONESb", [128, 128], BF16)
            self.ROTb = self.sb(es, "ROTb", [128, 128], BF16)
            self.epsc = self.sb(es, "epsc", [128, 1], F32)
            self.maskt = self.sb(es, "maskt", [128, NG], F32)
            self.r_junk = self.rot(es, "junk", [128, D], BF16, 2)
            self.r_ss = self.rot(es, "ss", [128, 4], F32, 4)
            self.setup()
            self.phase_0()
            for L in range(DEPTH):
                with ExitStack() as es2:
                    self.SC = self.sb(es2, "SC", [128, 2, 3, 4, NCH], F32)
                    self.phase_A(L)
                    self.phase_B(L)
                self.phase_C1(L)
                self.phase_C2a(L)
                self.phase_C2b(L)
                self.phase_D1(L)
                self.phase_D2(L)
            self.p.barrier()
            self.p.emit()
        return nc

    def setup(self):
        DEPTH = self.DEPTH
        self.dma(self.CST[:], self.cst[:, :], (), ["cstf"])
        self.dma(self.PP[:], self.ppd[:, :], (), ["pp"])
        self.dma(self.maskt[:], self.maskc[:, :], (), ["mask"])
        for dst, c0 in ((self.identb, 0), (self.BDb, 128), (self.ONESb, 256), (self.ROTb, 384)):
            self.cp(dst[:], self.CST[:, c0:c0 + 128], ["cstf"], ["cst"])
        self.ms(self.epsc[:], EPS, ["cst"])
        with ExitStack() as es:
            e = self.sb(es, "lbe", [128, 2, DEPTH * 4], F32)
            s = self.sb(es, "lbs", [128, 2, 4], F32)
            self.act(e[:], self.PP[:, 0:8 * DEPTH].rearrange("p (a n) -> p a n", a=2), AF.Exp, ["pp"], ["lbe"])
            self.cp(s[:], e[:, :, 0:4], ["lbe"], ["lbs"])
            for d in range(1, DEPTH):
                self.tt(s[:], s[:], e[:, :, d * 4:(d + 1) * 4], ALU.add, ["lbs", "lbe"], ["lbs"])
            self.p.op("dve", lambda en: en.reciprocal(s[:], s[:]), ["lbs"], ["lbs"])
            for d in range(DEPTH):
                self.tt(e[:, :, d * 4:(d + 1) * 4], e[:, :, d * 4:(d + 1) * 4], s[:], ALU.mult, ["lbs", "lbe"], ["lbe"])
            self.ms(self.LB[:, :, 0, 0:4], 0.0, ["LB"])
            for d in range(1, DEPTH):
                self.tt(self.LB[:, :, 0, d * 4:(d + 1) * 4], self.LB[:, :, 0, (d - 1) * 4:d * 4],
                        e[:, :, d * 4:(d + 1) * 4], ALU.add, ["LB", "lbe"], ["LB"])
            self.ts(self.LB[:, :, 1, :], self.LB[:, :, 0, :], -1.0, 1.0, ALU.mult, ALU.add, ["LB"], ["LB"])
            for L in range(DEPTH):
                c = self.lbase(L) + 1
                self.ts(self.PP[:, c:c + 1], self.PP[:, c:c + 1], 0.125, None, ALU.mult, ALU.bypass, ["pp"], ["pp"])
            z = self.sb(es, "zpad", [128, 8, 8], BF16)
            self.ms(z[:], 0.0, ["zpad"])
            UTv = self.UT.rearrange("(k p) n -> p k n", p=128)
            H2v = self.H2T.rearrange("(k p) n -> p k n", p=128)
            NS = self.NS
            self.dma(UTv[:, :, 0:8], z[:, 0:4, :], ["zpad"], [])
            self.dma(UTv[:, :, NS + 8:NS + 16], z[:, 0:4, :], ["zpad"], [])
            self.dma(H2v[:, :, 0:1], z[:, :, 0:1], ["zpad"], [], allow_slow_non_contiguous=True)
            self.dma(H2v[:, :, NS + 1:NS + 2], z[:, :, 0:1], ["zpad"], [], allow_slow_non_contiguous=True)
            self.p.barrier()

    def load_g(self, tile, key, row):
        self.dma(tile[:], self.gv[row:row + 1, :].partition_broadcast(128), (), [key])

    def phase_0(self):
        with ExitStack() as es:
            g0 = self.sb(es, "g0", [128, D], F32)
            self.load_g(g0, "g0", 0)
            xr = self.rot(es, "p0x", [128, D], F32, 3)
            hr = self.rot(es, "p0h", [128, D], BF16, 2)
            hTr = self.rot(es, "p0hT", [128, 8, 512], BF16, 2)
            H1v = self.H1T.rearrange("(k p) n -> p k n", p=128)
            for t in range(self.NT):
                hT, hTk = hTr.get()
                for s in range(4):
                    g = t * 4 + s
                    xt, xk = xr.get()
                    self.dma(xt[:], self.x[g * 128:(g + 1) * 128, :], (), [xk])
                    self.dma(self.XS[g * 128:(g + 1) * 128, :], xt[:], [xk], [])
                    h, hk = hr.get()
                    self.norm_rows(xt[:], xk, g0[:], "g0", self.maskt[:, g:g + 1], h[:], hk)
                    self.transpose_rows(h[:], hk, hT, hTk, s * 128)
                self.dma(H1v[:, :, t * 512:(t + 1) * 512], hT[:], [hTk], [])
            self.p.barrier()

    def phase_A(self, L):
        NT = self.NT
        lb = self.lbase(L)
        with ExitStack() as es:
            WA = self.sb(es, "WA", [128, 8, 4864], BF16)
            self.wload(WA, self.w_inA[L].rearrange("(k p) f -> p k f", p=128), "WA", 8)
            hTr = self.rot(es, "a_hT", [128, 8, 512], BF16, 2)
            csr = self.rot(es, "a_cs", [128, 2, 512], F32, 1)
            uTr = self.rot(es, "a_uT", [128, 4, 512], BF16, 1)
            qkr = {k: self.rot(es, "a_qk" + k[0] + k[1], [128, 4, 512], BF16, 1) for k in self.QK}
            sgr = self.rot(es, "a_sg", [128, 4, 512], BF16, 1)
            qtr = self.rot(es, "a_qt", [128, 8, 512], BF16, 1)
            ktr = self.rot(es, "a_kt", [128, 4, 512], BF16, 1)
            vhr = self.rot(es, "a_vh", [128, 4, 512], BF16, 1)
            var = self.rot(es, "a_va", [128, 4, 260], BF16, 1)
            qsr = self.rot(es, "a_qs", [128, 512], F32, 2)
            f32r = self.rot(es, "a_f", [128, 512], F32, 10)
            b16r = self.rot(es, "a_b", [128, 512], BF16, 4)
            smr = self.rot(es, "a_sm", [128, 8], F32, 4)
            H1v = self.H1T.rearrange("(k p) n -> p k n", p=128)
            reset = self.CST[:, 768:1280]

            def proj(hT, hTk, col0):
                b, bk = self.bank()
                ps = self.psb(b)
                for k in range(8):
                    self.mm(ps, WA[:, k, col0:col0 + 128], hT[:, k, :], k == 0, k == 7, ["WA", hTk], [bk])
                return ps, bk

            for t in range(NT):
                c0 = t * 512
                hT, hTk = hTr.get()
                self.dma(hT[:], H1v[:, :, c0:c0 + 512], (), [hTk])
                cs, csk = csr.get()
                self.dma(cs[:, 0, :], self.cosT[:, c0:c0 + 512], (), [csk])
                self.dma(cs[:, 1, :], self.sinT[:, c0:c0 + 512], (), [csk])
                uT, uTk = uTr.get()
                for g in range(4):
                    ps, bk = proj(hT, hTk, g * 128)
                    self.cp(uT[:, g, :], ps, [bk], [uTk], eng="act")
                self.dma(self.UT.rearrange("(k p) n -> p k n", p=128)[:, :, 8 + c0:8 + c0 + 512], uT[:], [uTk], [])
                st = {k: qkr[k].get() for k in self.QK}
                sg, sgk = sgr.get()
                for h in range(4):
                    ps, bk = proj(hT, hTk, 512 + h * 128)
                    qs, qsk = qsr.get()
                    self.act(qs[:], ps, AF.Copy, [bk], [qsk], scale=128.0 ** -0.5)
                    for di, d in enumerate("fb"):
                        ps, bk = proj(hT, hTk, 1024 + di * 512 + h * 128)
                        f, fk = f32r.get()
                        self.act(f[:], ps, AF.Sigmoid, [bk], [fk])
                        lbc = self.LB[:, di, 0, L * 4 + h:L * 4 + h + 1]
                        omc = self.LB[:, di, 1, L * 4 + h:L * 4 + h + 1]
                        self.ts(f[:], f[:], omc, lbc, ALU.mult, ALU.add, [fk, "LB"], [fk])
                        lf, lfk = f32r.get()
                        self.act(lf[:], f[:], AF.Ln, [fk], [lfk])
                        kk, kkk = f32r.get()
                        self.ts(kk[:], f[:], -1.0, 1.0, ALU.mult, ALU.add, [fk], [kkk])
                        cum, cumk = f32r.get()
                        self.p.op("dve", lambda e, cum=cum, lf=lf: e.tensor_tensor_scan(
                            cum[:], reset, lf[:], 0.0, ALU.mult, ALU.add), [lfk, "cstf"], [cumk])
                        A, Ak = f32r.get()
                        sm, smk = smr.get()
                        c3 = lambda tl: tl[:].rearrange("p (n j) -> p n j", j=64)
                        Lc = cum[:, 63:512:64]
                        scd = lambda j: self.SC[:, di, j, h, t * 8:(t + 1) * 8]
                        sck = "SC"
                        if d == "f":
                            m = cum[:, 31:512:64]
                            self.tt(c3(A), c3(cum), m.unsqueeze(2).to_broadcast([128, 8, 64]), ALU.subtract, [cumk], [Ak])
                            self.tt(sm[:], Lc, m, ALU.subtract, [cumk], [smk])
                            self.act(scd(0), m, AF.Exp, [cumk], [sck])
                            self.act(scd(2), sm[:], AF.Exp, [smk], [sck])
                        else:
                            E, Ek = f32r.get()
                            self.tt(E[:], cum[:], lf[:], ALU.subtract, [cumk, lfk], [Ek])
                            m = E[:, 31:512:64]
                            self.tt(c3(A), m.unsqueeze(2).to_broadcast([128, 8, 64]), c3(E), ALU.subtract, [Ek], [Ak])
                            self.tt(sm[:], Lc, m, ALU.subtract, [cumk, Ek], [smk])
                            self.act(scd(0), sm[:], AF.Exp, [smk], [sck])
                            self.act(scd(2), m, AF.Exp, [Ek], [sck])
                        self.act(scd(1), Lc, AF.Exp, [cumk], [sck])
                        eA, eAk = f32r.get()
                        self.act(eA[:], A[:], AF.Exp, [Ak], [eAk])
                        qt_, qtk_ = st[(d, "q")]
                        self.tt(qt_[:, h, :], qs[:], eA[:], ALU.mult, [qsk, eAk], [qtk_])
                        eN, eNk = f32r.get()
                        self.act(eN[:], A[:], AF.Exp, [Ak], [eNk], scale=-1.0)
                        kt_, ktk_ = st[(d, "k")]
                        self.tt(kt_[:, h, :], kk[:], eN[:], ALU.mult, [kkk, eNk], [ktk_])
                    ps, bk = proj(hT, hTk, 2048 + h * 128)
                    self.act(sg[:, h, :], ps, AF.Silu, [bk], [sgk])
                for k in self.QK:
                    tl, tk = st[k]
                    self.dma(self.QK[k].rearrange("(h p) n -> p h n", p=128)[:, :, c0:c0 + 512], tl[:], [tk], [])
                self.dma(self.SG.rearrange("(h p) n -> p h n", p=128)[:, :, c0:c0 + 512], sg[:], [sgk], [])
                qt, qtk = qtr.get()
                kt, ktk = ktr.get()
                for oc in range(12):
                    ps, bk = proj(hT, hTk, 2560 + oc * 128)
                    gcol = self.ppc(lb + 1) if oc < 8 else self.ppc(lb + 2)
                    sq, sqk = b16r.get()
                    self.act(sq[:], ps, AF.Square, [bk], [sqk])
                    b2, bk2 = self.bank()
                    self.mm(self.psb(b2), self.BDb[:], sq[:], True, True, ["cst", sqk], [bk2])
                    r1, r1k = f32r.get()
                    self.act(r1[:], self.psb(b2), AF.Ln, [bk2], [r1k], bias=self.epsc[:, 0:1], scale=1.0 / 64)
                    self.act(r1[:], r1[:], AF.Exp, [r1k], [r1k], scale=-0.5)
                    zn, znk = b16r.get()
                    self.stt(zn[:], ps, gcol, r1[:], ALU.mult, ALU.mult, [bk, "pp", r1k], [znk])
                    b3, bk3 = self.bank()
                    self.mm(self.psb(b3), self.ROTb[:], zn[:], True, True, ["cst", znk], [bk3])
                    t2, t2k = f32r.get()
                    self.tt(t2[:], zn[:], cs[:, 0, :], ALU.mult, [znk, csk], [t2k])
                    t3, t3k = f32r.get()
                    self.tt(t3[:], self.psb(b3), cs[:, 1, :], ALU.mult, [bk3, csk], [t3k])
                    if oc < 8:
                        self.tt(qt[:, oc, :], t2[:], t3[:], ALU.add, [t2k, t3k], [qtk], eng="pool")
                    else:
                        self.tt(kt[:, oc - 8, :], t2[:], t3[:], ALU.add, [t2k, t3k], [ktk], eng="pool")
                self.dma(self.QT.rearrange("(k p) n -> p k n", p=128)[:, :, c0:c0 + 512], qt[:], [qtk], [])
                self.dma(self.KT2.rearrange("(k p) n -> p k n", p=128)[:, :, c0:c0 + 512], kt[:], [ktk], [])
                vh, vhk = vhr.get()
                va, vak = var.get()
                for s in range(4):
                    g = t * 4 + s
                    b, bk = self.bank()
                    for k in range(8):
                        self.mm(self.psb(b), hT[:, k, s * 128:(s + 1) * 128], WA[:, k, 4096:4608], k == 0, k == 7, ["WA", hTk], [bk])
                    self.cp(vh[:, s, :], self.psb(b), [bk], [vhk], eng="act")
                    b, bk = self.bank()
                    for k in range(8):
                        self.mm(self.psb(b, 256), hT[:, k, s * 128:(s + 1) * 128], WA[:, k, 4608:4864], k == 0, k == 7, ["WA", hTk], [bk])
                    vav = va[:, s, :].rearrange("p (h c) -> p h c", c=65)
                    self.cp(vav[:, :, 0:64], self.psb(b, 256).rearrange("p (h c) -> p h c", c=64), [bk], [vak])
                    self.cp(vav[:, :, 64:65], self.maskt[:, g:g + 1].unsqueeze(1).to_broadcast([128, 4, 1]), ["mask"], [vak])
                self.dma(self.VH.rearrange("(s p) f -> p s f", p=128)[:, t * 4:(t + 1) * 4, :], vh[:], [vhk], [])
                self.dma(self.VA.rearrange("(s p) f -> p s f", p=128)[:, t * 4:(t + 1) * 4, :], va[:], [vak], [])
            self.p.barrier()

    def phase_B(self, L):
        NT = self.NT
        with ExitStack() as es:
            Sf = self.sb(es, "b_Sf", [128, 2, 4, 128], F32)
            Sb = self.sb(es, "b_Sb", [128, 2, 4, 128], BF16)
            self.ms(Sf[:], 0.0, ["Sf00", "Sf01", "Sf02", "Sf03", "Sf10", "Sf11", "Sf12", "Sf13"])
            self.ms(Sb[:], 0.0, ["Sb00", "Sb01", "Sb02", "Sb03", "Sb10", "Sb11", "Sb12", "Sb13"])
            qr = {d: self.rot(es, "b_q" + d, [128, 4, 512], BF16, 2) for d in "fb"}
            kr = {d: self.rot(es, "b_k" + d, [128, 4, 512], BF16, 2) for d in "fb"}
            vr = {d: self.rot(es, "b_v" + d, [128, 4, 512], BF16, 2) for d in "fb"}
            ktr = {d: self.rot(es, "b_kt" + d, [128, 4, 4, 128], BF16, 2) for d in "fb"}
            otr = {d: self.rot(es, "b_o" + d, [128, 4, 512], F32, 2) for d in "fb"}
            smr = self.rot(es, "b_sm", [128, 128], BF16, 8)
            tmr = self.rot(es, "b_tm", [128, 128], F32, 8)
            MK = {"f": self.CST[:, 512:640], "b": self.CST[:, 640:768]}
            for step in range(NT):
                tiles = {"f": step, "b": NT - 1 - step}
                cur = {}
                for di, d in enumerate("fb"):
                    t = tiles[d]
                    c0 = t * 512
                    q, qk = qr[d].get()
                    k, kk = kr[d].get()
                    v, vk = vr[d].get()
                    self.dma(q[:], self.QK[(d, "q")].rearrange("(h p) n -> p h n", p=128)[:, :, c0:c0 + 512], (), [qk])
                    self.dma(k[:], self.QK[(d, "k")].rearrange("(h p) n -> p h n", p=128)[:, :, c0:c0 + 512], (), [kk])
                    self.dma(v[:], self.VH.rearrange("(s p) f -> p s f", p=128)[:, t * 4:(t + 1) * 4, :], (), [vk])
                    ktk_t, ktk = ktr[d].get()
                    for h in range(4):
                        b, bk = self.bank(4, 8)
                        psv = self.psb(b).bitcast(BF16)
                        for bl in range(4):
                            self.tr(psv[:, bl * 128:(bl + 1) * 128], k[:, h, bl * 128:(bl + 1) * 128], self.identb[:], [kk, "cst"], [bk])
                        self.cp(ktk_t[:, h, :, :], psv[:, 0:512].rearrange("p (b n) -> p b n", b=4), [bk], [ktk], eng="act")
                    o, ok = otr[d].get()
                    cur[d] = (t, q, qk, k, kk, v, vk, ktk_t, ktk, o, ok)
                for bi in range(4):
                    for di, d in enumerate("fb"):
                        t, q, qk, k, kk, v, vk, ktk_t, ktk, o, ok = cur[d]
                        blk = bi if d == "f" else 3 - bi
                        order = (0, 1) if d == "f" else (1, 0)
                        bc = slice(blk * 128, (blk + 1) * 128)
                        sms = []
                        for h in range(4):
                            b, bk = self.bank(4, 8)
                            ps = self.psb(b, 128)
                            self.mm(ps, k[:, h, bc], q[:, h, bc], True, True, [kk, qk], [bk])
                            sm, smk = smr.get()
                            self.tt(sm[:], ps, MK[d], ALU.mult, [bk, "cstf"], [smk])
                            sms.append((sm, smk))

                        def upd(h, ca):
                            gch = t * 8 + blk * 2 + ca
                            sfk, sbk = "Sf%d%d" % (di, h), "Sb%d%d" % (di, h)
                            b2, bk2 = self.bank(4, 8)
                            rows = slice(ca * 64, (ca + 1) * 64)
                            self.mm(self.psb(b2, 128), ktk_t[rows, h, blk, :], v[rows, blk, h * 128:(h + 1) * 128], True, True, [ktk, vk], [bk2])
                            tm, tmk = tmr.get()
                            self.ts(tm[:], Sf[:, di, h, :], self.SC[:, di, 1, h, gch:gch + 1], None, ALU.mult, ALU.bypass, [sfk, "SC"], [tmk])
                            self.stt(Sf[:, di, h, :], self.psb(b2, 128), self.SC[:, di, 2, h, gch:gch + 1], tm[:], ALU.mult, ALU.add, [bk2, "SC", tmk], [sfk])
                            nxt = gch + 1 if d == "f" else gch - 1
                            if 0 <= nxt < self.NCH:
                                self.ts(Sb[:, di, h, :], Sf[:, di, h, :], self.SC[:, di, 0, h, nxt:nxt + 1], None, ALU.mult, ALU.bypass, [sfk, "SC"], [sbk])

                        for h in range(4):
                            sm, smk = sms[h]
                            okey = "ps%d" % h
                            O = self.psb(h, 128)
                            ca = order[0]
                            self.mm(O, v[:, blk, h * 128:(h + 1) * 128], sm[:], True, False, [vk, smk], [okey])
                            self.mm(O[:, ca * 64:(ca + 1) * 64], Sb[:, di, h, :], q[:, h, blk * 128 + ca * 64:blk * 128 + (ca + 1) * 64],
                                    False, False, ["Sb%d%d" % (di, h), qk], [okey])
                            upd(h, ca)
                        for h in range(4):
                            okey = "ps%d" % h
                            O = self.psb(h, 128)
                            cb = order[1]
                            self.mm(O[:, cb * 64:(cb + 1) * 64], Sb[:, di, h, :], q[:, h, blk * 128 + cb * 64:blk * 128 + (cb + 1) * 64],
                                    False, True, ["Sb%d%d" % (di, h), qk], [okey])
                            upd(h, cb)
                            self.cp(o[:, h, bc], O, [okey], [ok], eng="act")
                for d in "fb":
                    t, q, qk, k, kk, v, vk, ktk_t, ktk, o, ok = cur[d]
                    self.dma(self.OO[d].rearrange("(h p) n -> p h n", p=128)[:, :, t * 512:(t + 1) * 512], o[:], [ok], [])
            self.p.barrier()

    def phase_C1(self, L):
        NS, NT, NG = self.NS, self.NT, self.NG
        with ExitStack() as es:
            KT = self.sb(es, "c_KT", [128, 4, NS], BF16)
            for k in range(4):
                self.dma(KT[:, k, :], self.KT2.rearrange("(k p) n -> p k n", p=128)[:, k, :], (), ["KT"])
            VAt = self.sb(es, "c_VA", [128, NG, 260], BF16)
            VAv = self.VA.rearrange("(g p) f -> p g f", p=128)
            for g0 in range(0, NG, 16):
                g1 = min(NG, g0 + 16)
                self.dma(VAt[:, g0:g1, :], VAv[:, g0:g1, :], (), ["VAt"])
            qr = self.rot(es, "c_q", [128, 8, 512], BF16, 2)
            ptr = self.rot(es, "c_pt", [128, 1024], BF16, 3)
            yar = self.rot(es, "c_ya", [64, 16, 512], BF16, 2)
            rdr = self.rot(es, "c_rd", [128, 512], F32, 2)
            bcr = self.rot(es, "c_bc", [64, 512], F32, 2)
            ONESf = self.CST[:, 256:384]
            sbi = 0
            bci = 0
            for qt in range(NT):
                q, qk = qr.get()
                self.dma(q[:], self.QT.rearrange("(k p) n -> p k n", p=128)[:, :, qt * 512:(qt + 1) * 512], (), [qk])
                ya, yak = yar.get()
                for c in range(8):
                    kvh = c // 2
                    for kt in range(NG):
                        s0 = (sbi % 2) * 2
                        sbi += 1
                        kc = slice(kt * 128, (kt + 1) * 128)
                        self.mm(self.psb(s0), KT[0:64, kvh, kc], q[0:64, c, :], True, True, ["KT", qk], ["ps%d" % s0])
                        self.mm(self.psb(s0 + 1), KT[64:128, kvh, kc], q[64:128, c, :], True, True, ["KT", qk], ["ps%d" % (s0 + 1)])
                        pt, ptk = ptr.get()
                        self.act(pt[:], self.PS[:, s0 * 512:s0 * 512 + 1024], AF.Exp, ["ps%d" % s0, "ps%d" % (s0 + 1)], [ptk])
                        lhs = VAt[:, kt, kvh * 65:(kvh + 1) * 65]
                        self.mm(self.PS[0:65, 4 * 512:5 * 512], lhs, pt[:, 0:512], kt == 0, kt == NG - 1, ["VAt", ptk], ["ps4"])
                        self.mm(self.PS[0:65, 5 * 512:6 * 512], lhs, pt[:, 512:1024], kt == 0, kt == NG - 1, ["VAt", ptk], ["ps5"])
                    for j in range(2):
                        h = 2 * c + j
                        ob = 4 + j
                        okey = "ps%d" % ob
                        rd, rdk = rdr.get()
                        self.p.op("dve", lambda e, rd=rd, ob=ob: e.reciprocal(rd[64:65, :], self.PS[64:65, ob * 512:(ob + 1) * 512]), [okey], [rdk])
                        bb = 6 + (bci % 2)
                        bci += 1
                        bkey = "ps%d" % bb
                        self.mm(self.PS[0:64, bb * 512:(bb + 1) * 512], ONESf[64:65, 0:64], rd[64:65, :], True, True, ["cstf", rdk], [bkey])
                        bc, bck = bcr.get()
                        self.cp(bc[:], self.PS[0:64, bb * 512:(bb + 1) * 512], [bkey], [bck])
                        self.tt(ya[:, h, :], self.PS[0:64, ob * 512:(ob + 1) * 512], bc[:], ALU.mult, [okey, bck], [yak])
                self.dma(self.YA.rearrange("(h d) n -> d h n", d=64)[:, :, qt * 512:(qt + 1) * 512], ya[:], [yak], [])
            self.p.barrier()

    def phase_C2a(self, L):
        NT = self.NT
        lb = self.lbase(L)
        with ExitStack() as es:
            WG = self.sb(es, "e_WG", [128, 8, 3072], BF16)
            self.wload(WG, self.w_inG[L].rearrange("(k p) f -> p k f", p=128), "WG", 8)
            WP = self.sb(es, "e_WP", [128, 4, 128], BF16)
            self.dma(WP[:], self.poolw[L], (), ["WP"], q="pool")
            WBP = self.sb(es, "e_WBP", [128, 4, D], BF16)
            self.wload(WBP, self.wbp[L].rearrange("(k p) f -> p k f", p=128), "WBP", 4)
            WBH = self.sb(es, "e_WBH", [128, 4, D], BF16)
            self.wload(WBH, self.wbh[L].rearrange("(k p) f -> p k f", p=128), "WBH", 4)
            WBA = self.sb(es, "e_WBA", [128, 8, D], BF16)
            self.wload(WBA, self.wba[L].rearrange("(k p) f -> p k f", p=128), "WBA", 8)
            hTr = self.rot(es, "e_hT", [128, 8, 512], BF16, 1)
            ur = self.rot(es, "e_u", [128, 4, 528], BF16, 1)
            oor = self.rot(es, "e_oo", [128, 512], F32, 4)
            sgr = self.rot(es, "e_sg", [128, 4, 512], BF16, 1)
            yar = self.rot(es, "e_ya", [128, 8, 512], BF16, 1)
            rcr = self.rot(es, "e_rc", [128, 4, 512], F32, 1)
            ypr = self.rot(es, "e_yp", [128, 4, 512], BF16, 1)
            yhr = self.rot(es, "e_yh", [128, 4, 512], BF16, 1)
            mgr = self.rot(es, "e_mg", [128, 8, 512], BF16, 1)
            gtr = self.rot(es, "e_gt", [128, 512], F32, 3)
            f32r = self.rot(es, "e_f", [128, 528], F32, 6)
            b16r = self.rot(es, "e_b", [128, 512], BF16, 2)
            for t in range(NT):
                c0 = t * 512
                hT, hTk = hTr.get()
                self.dma(hT[:], self.H1T.rearrange("(k p) n -> p k n", p=128)[:, :, c0:c0 + 512], (), [hTk])
                u, uk = ur.get()
                self.dma(u[:], self.UT.rearrange("(k p) n -> p k n", p=128)[:, :, c0:c0 + 528], (), [uk])
                sg, sgk = sgr.get()
                self.dma(sg[:], self.SG.rearrange("(h p) n -> p h n", p=128)[:, :, c0:c0 + 512], (), [sgk])
                ya, yak = yar.get()
                self.dma(ya[:], self.YA.rearrange("(k p) n -> p k n", p=128)[:, :, c0:c0 + 512], (), [yak])
                rct, rck = rcr.get()
                for g in range(4):
                    self.dma(rct[:, g, :], self.rc[g:g + 1, c0:c0 + 512].partition_broadcast(128), (), [rck])
                yp, ypk = ypr.get()
                for g in range(4):
                    S, Sk = f32r.get()
                    self.tt(S[:, 1:528], u[:, g, 0:527], u[:, g, 1:528], ALU.add, [uk], [Sk], eng="pool")
                    lo, hi = 1, 528
                    for lv in range(g):
                        sh = 1 << lv
                        S2, S2k = f32r.get()
                        nlo, nhi = lo + sh, hi - sh
                        self.tt(S2[:, nlo:nhi], S[:, nlo - sh:nhi - sh], S[:, nlo + sh:nhi + sh], ALU.add, [Sk], [S2k], eng="pool")
                        S, Sk, lo, hi = S2, S2k, nlo, nhi
                    m1, m1k = f32r.get()
                    self.tt(m1[:, 0:512], S[:, 8:520], rct[:, g, :], ALU.mult, [Sk, rck], [m1k])
                    mx, mxk = b16r.get()
                    self.tt(mx[:], m1[:, 0:512], u[:, g, 8:520], ALU.subtract, [m1k, uk], [mxk])
                    b, bk = self.bank()
                    self.mm(self.psb(b), WP[:, g, :], mx[:], True, True, ["WP", mxk], [bk])
                    self.act(yp[:, g, :], self.psb(b), AF.Identity, [bk, "pp"], [ypk], scale=self.ppc(lb + 3 + g), bias=0.0)
                yh, yhk = yhr.get()
                for h in range(4):
                    of, ofk = oor.get()
                    ob, obk = oor.get()
                    self.dma(of[:], self.OO["f"][h * 128:(h + 1) * 128, c0:c0 + 512], (), [ofk])
                    self.dma(ob[:], self.OO["b"][h * 128:(h + 1) * 128, c0:c0 + 512], (), [obk])
                    self.tt(of[:], of[:], ob[:], ALU.add, [ofk, obk], [ofk])
                    sq, sqk = b16r.get()
                    self.act(sq[:], of[:], AF.Square, [ofk], [sqk])
                    b, bk = self.bank()
                    self.mm(self.psb(b), self.ONESb[:], sq[:], True, True, ["cst", sqk], [bk])
                    r1, r1k = f32r.get()
                    self.act(r1[:, 0:512], self.psb(b), AF.Ln, [bk], [r1k], bias=self.epsc[:, 0:1], scale=1.0 / 128)
                    self.act(r1[:, 0:512], r1[:, 0:512], AF.Exp, [r1k], [r1k], scale=-0.5)
                    self.stt(ob[:], of[:], self.ppc(lb), r1[:, 0:512], ALU.mult, ALU.mult, [ofk, "pp", r1k], [obk])
                    self.tt(yh[:, h, :], ob[:], sg[:, h, :], ALU.mult, [obk, sgk], [yhk])
                mg, mgk = mgr.get()
                for oc in range(8):
                    ocs = slice(oc * 128, (oc + 1) * 128)
                    bp, bpk = self.bank()
                    for k in range(4):
                        self.mm(self.psb(bp), WBP[:, k, ocs], yp[:, k, :], k == 0, k == 3, ["WBP", ypk], [bpk])
                    bh, bhk = self.bank()
                    for k in range(4):
                        self.mm(self.psb(bh), WBH[:, k, ocs], yh[:, k, :], k == 0, k == 3, ["WBH", yhk], [bhk])
                    ba, bak = self.bank()
                    for k in range(8):
                        self.mm(self.psb(ba), WBA[:, k, ocs], ya[:, k, :], k == 0, k == 7, ["WBA", yak], [bak])
                    ms = []
                    for j, (bb, bbk) in enumerate(((bp, bpk), (bh, bhk), (ba, bak))):
                        bg, bgk = self.bank()
                        for k in range(8):
                            self.mm(self.psb(bg), WG[:, k, j * D + oc * 128:j * D + (oc + 1) * 128], hT[:, k, :], k == 0, k == 7, ["WG", hTk], [bgk])
                        gt, gtk = gtr.get()
                        self.act(gt[:], self.psb(bg), AF.Sigmoid, [bgk], [gtk])
                        self.tt(gt[:], gt[:], self.psb(bb), ALU.mult, [gtk, bbk], [gtk])
                        ms.append((gt, gtk))
                    self.tt(ms[0][0][:], ms[0][0][:], ms[1][0][:], ALU.add, [ms[0][1], ms[1][1]], [ms[0][1]], eng="pool")
                    self.tt(mg[:, oc, :], ms[0][0][:], ms[2][0][:], ALU.add, [ms[0][1], ms[2][1]], [mgk], eng="pool")
                self.dma(self.MG.rearrange("(k p) n -> p k n", p=128)[:, :, c0:c0 + 512], mg[:], [mgk], [])
            self.p.barrier()

    def phase_C2b(self, L):
        NT = self.NT
        with ExitStack() as es:
            WO = self.sb(es, "f_WO", [128, 8, D], BF16)
            self.wload(WO, self.wout[L].rearrange("(k p) f -> p k f", p=128), "WO", 8)
            gf = self.sb(es, "f_g", [128, D], F32)
            self.load_g(gf, "gf", 3 * L + 1)
            mgr = self.rot(es, "f_mg", [128, 8, 512], BF16, 2)
            xr = self.rot(es, "f_x", [128, D], F32, 3)
            hr = self.rot(es, "f_h", [128, D], BF16, 2)
            hTr = self.rot(es, "f_hT", [128, 8, 512], BF16, 2)
            for t in range(NT):
                c0 = t * 512
                mg, mgk = mgr.get()
                self.dma(mg[:], self.MG.rearrange("(k p) n -> p k n", p=128)[:, :, c0:c0 + 512], (), [mgk])
                hT, hTk = hTr.get()
                for s in range(4):
                    g = t * 4 + s
                    x, xk = xr.get()
                    self.dma(x[:], self.XS[g * 128:(g + 1) * 128, :], (), [xk])
                    for fh in range(2):
                        b, bk = self.bank()
                        for k in range(8):
                            self.mm(self.psb(b), mg[:, k, s * 128:(s + 1) * 128], WO[:, k, fh * 512:(fh + 1) * 512], k == 0, k == 7, [mgk, "WO"], [bk])
                        self.tt(x[:, fh * 512:(fh + 1) * 512], x[:, fh * 512:(fh + 1) * 512], self.psb(b), ALU.add, [xk, bk], [xk])
                    self.dma(self.XS[g * 128:(g + 1) * 128, :], x[:], [xk], [])
                    h, hk = hr.get()
                    self.norm_rows(x[:], xk, gf[:], "gf", self.maskt[:, g:g + 1], h[:], hk)
                    self.transpose_rows(h[:], hk, hT, hTk, s * 128)
                self.dma(self.H2T.rearrange("(k p) n -> p k n", p=128)[:, :, 1 + c0:1 + c0 + 512], hT[:], [hTk], [])
            self.p.barrier()

    def phase_D1(self, L):
        NS = self.NS
        lb = self.lbase(L)
        cw = lb + 7
        with ExitStack() as es:
            WU = self.sb(es, "g_WU", [128, 8, 2 * DFF], BF16)
            self.wload(WU, self.wup[L].rearrange("(k p) f -> p k f", p=128), "WU", 8)
            WD = self.sb(es, "g_WD", [128, 22, D], BF16)
            self.wload(WD, self.wdn[L].rearrange("(k p) f -> p k f", p=128), "WD", 22)
            hTr = self.rot(es, "g_hT", [128, 8, 258], BF16, 2)
            acr = self.rot(es, "g_ac", [128, 22, 256], BF16, 1)
            xr = self.rot(es, "g_x", [128, D], F32, 2)
            f32r = self.rot(es, "g_f", [128, 256], F32, 8)

            def conv(ps, bk, ci):
                t1, t1k = f32r.get()
                self.act(t1[:], ps[:, 1:257], AF.Identity, [bk, "pp"], [t1k], scale=self.ppc(cw + NFC + ci), bias=self.ppc(cw + 3 * NFC + ci))
                self.stt(t1[:], ps[:, 0:256], self.ppc(cw + ci), t1[:], ALU.mult, ALU.add, [bk, "pp", t1k], [t1k])
                self.stt(t1[:], ps[:, 2:258], self.ppc(cw + 2 * NFC + ci), t1[:], ALU.mult, ALU.add, [bk, "pp", t1k], [t1k])
                return t1, t1k

            for t in range(NS // 256):
                c0 = t * 256
                hT, hTk = hTr.get()
                self.dma(hT[:], self.H2T.rearrange("(k p) n -> p k n", p=128)[:, :, c0:c0 + 258], (), [hTk])
                ac, ack = acr.get()
                for j in range(22):
                    res = []
                    for ci in (j, 22 + j):
                        b, bk = self.bank()
                        ps = self.psb(b, 258)
                        for k in range(8):
                            self.mm(ps, WU[:, k, ci * 128:(ci + 1) * 128], hT[:, k, :], k == 0, k == 7, ["WU", hTk], [bk])
                        res.append(conv(ps, bk, ci))
                    (a, ak), (gg, ggk) = res
                    x2, x2k = f32r.get()
                    self.tt(x2[:], a[:], a[:], ALU.mult, [ak], [x2k], eng="pool")
                    self.ts(x2[:], x2[:], 0.044715, 1.0, ALU.mult, ALU.add, [x2k], [x2k], eng="pool")
                    self.tt(x2[:], x2[:], a[:], ALU.mult, [x2k, ak], [x2k], eng="pool")
                    self.act(x2[:], x2[:], AF.Sigmoid, [x2k], [x2k], scale=1.5957691216057308)
                    self.tt(a[:], a[:], x2[:], ALU.mult, [ak, x2k], [ak])
                    self.tt(ac[:, j, :], a[:], gg[:], ALU.mult, [ak, ggk], [ack])
                for s in range(2):
                    r0 = c0 + s * 128
                    x, xk = xr.get()
                    self.dma(x[:], self.XS[r0:r0 + 128, :], (), [xk])
                    for fh in range(2):
                        b, bk = self.bank()
                        for k in range(22):
                            self.mm(self.psb(b), ac[:, k, s * 128:(s + 1) * 128], WD[:, k, fh * 512:(fh + 1) * 512], k == 0, k == 21, [ack, "WD"], [bk])
                        self.tt(x[:, fh * 512:(fh + 1) * 512], x[:, fh * 512:(fh + 1) * 512], self.psb(b), ALU.add, [xk, bk], [xk])
                    self.dma(self.XS[r0:r0 + 128, :], x[:], [xk], [])
            self.p.barrier()

    def phase_D2(self, L):
        NT = self.NT
        last = L == self.DEPTH - 1
        with ExitStack() as es:
            WPG = self.sb(es, "h_WPG", [128, 8, D], BF16)
            self.wload(WPG, self.wpg[L].rearrange("(k p) f -> p k f", p=128), "WPG", 8)
            WPL = self.sb(es, "h_WPL", [128, 2, D], BF16)
            self.wload(WPL, self.wpl[L].rearrange("(k p) f -> p k f", p=128), "WPL", 2)
            gp = self.sb(es, "h_gp", [128, D], F32)
            self.load_g(gp, "gp", 3 * L + 2)
            gn = self.sb(es, "h_gn", [128, D], F32)
            self.load_g(gn, "gn", 3 * self.DEPTH if last else 3 * (L + 1))
            ptr = self.rot(es, "h_pT", [128, 2, 512], BF16, 2)
            xr = self.rot(es, "h_x", [128, D], F32, 3)
            hr = self.rot(es, "h_h", [128, D], BF16, 2)
            h3r = self.rot(es, "h_h3T", [128, 8, 128], BF16, 2)
            hTr = self.rot(es, "h_hT", [128, 8, 512], BF16, 2)
            yr = self.rot(es, "h_y", [128, D], F32, 2)
            gsr = self.rot(es, "h_gs", [128, 512], F32, 3)
            for t in range(NT):
                c0 = t * 512
                pt, ptk = ptr.get()
                self.dma(pt[:], self.pT[L].rearrange("(k p) n -> p k n", p=128)[:, :, c0:c0 + 512], (), [ptk], q="pool")
                hT, hTk = hTr.get()
                for s in range(4):
                    g = t * 4 + s
                    x, xk = xr.get()
                    self.dma(x[:], self.XS[g * 128:(g + 1) * 128, :], (), [xk])
                    h, hk = hr.get()
                    self.norm_rows(x[:], xk, gp[:], "gp", None, h[:], hk)
                    h3, h3k = h3r.get()
                    self.transpose_rows(h[:], hk, h3, h3k, 0)
                    for fh in range(2):
                        fs = slice(fh * 512, (fh + 1) * 512)
                        b, bk = self.bank()
                        for k in range(8):
                            self.mm(self.psb(b), h3[:, k, :], WPG[:, k, fs], k == 0, k == 7, [h3k, "WPG"], [bk])
                        gs, gsk = gsr.get()
                        self.act(gs[:], self.psb(b), AF.Sigmoid, [bk], [gsk])
                        b2, bk2 = self.bank()
                        for k in range(2):
                            self.mm(self.psb(b2), pt[:, k, s * 128:(s + 1) * 128], WPL[:, k, fs], k == 0, k == 1, [ptk, "WPL"], [bk2])
                        self.tt(gs[:], gs[:], self.psb(b2), ALU.mult, [gsk, bk2], [gsk])
                        self.tt(x[:, fs], x[:, fs], gs[:], ALU.add, [xk, gsk], [xk])
                    if last:
                        yt, ytk = yr.get()
                        self.norm_rows(x[:], xk, gn[:], "gn", None, yt[:], ytk)
                        self.dma(self.y[g * 128:(g + 1) * 128, :], yt[:], [ytk], [])
                    else:
                        self.dma(self.XS[g * 128:(g + 1) * 128, :], x[:], [xk], [])
                        h, hk = hr.get()
                        self.norm_rows(x[:], xk, gn[:], "gn", self.maskt[:, g:g + 1], h[:], hk)
                        self.transpose_rows(h[:], hk, hT, hTk, s * 128)
                if not last:
                    self.dma(self.H1T.rearrange("(k p) n -> p k n", p=128)[:, :, c0:c0 + 512], hT[:], [hTk], [])
            self.p.barrier()


def _consts(NS, n_real):
    cst = np.zeros((128, 1280), np.float32)
    idx = np.arange(128)
    cst[:, 0:128] = np.eye(128, dtype=np.float32)
    cst[:, 128:256] = (idx[:, None] // 64 == idx[None, :] // 64)
    cst[:, 256:384] = 1.0
    rot = np.zeros((128, 128), np.float32)
    for m in range(128):
        if (m % 32) < 16:
            rot[m + 16, m] = -1.0
        else:
            rot[m - 16, m] = 1.0
    cst[:, 384:512] = rot
    same = idx[:, None] // 64 == idx[None, :] // 64
    cst[:, 512:640] = same & (idx[:, None] <= idx[None, :])
    cst[:, 640:768] = same & (idx[:, None] >= idx[None, :])
    cst[:, 768:1280] = (np.arange(512) % 64 != 0)[None, :]
    t = np.arange(NS)
    maskc = (t < n_real).astype(np.float32).reshape(NS // 128, 128).T.copy()
    rc = np.ones((4, NS), np.float32)
    for g, w in enumerate((2, 4, 8, 16)):
        lo = np.clip(t - w // 2, 0, n_real)
        hi = np.clip(t + w // 2, 0, n_real)
        c = np.maximum(hi - lo, 1).astype(np.float32)
        rc[g] = np.where(t < n_real, np.float32(1.0) / c, np.float32(1.0))
    inv_freq = (1.0 / (np.float32(10000.0) ** (np.arange(0, 32, 2, dtype=np.float32) / np.float32(32)))).astype(np.float32)
    d = idx % 64
    a = d // 32
    i = d % 16
    pos = np.where(a[:, None] == 0, (t // 64)[None, :], (t % 64)[None, :]).astype(np.float32)
    ang = pos * inv_freq[i][:, None]
    return cst, maskc, rc, np.cos(ang).astype(np.float32), np.sin(ang).astype(np.float32)


def _shared(inp, DEPTH):
    f = lambda k: np.asarray(inp[k], np.float32)
    w_in = f("w_in")
    cols = list(range(0, 2048)) + list(range(2560, 3072)) + list(range(3072, 4096))
    for kvh in range(4):
        cols += list(range(4096 + kvh * 64, 4096 + (kvh + 1) * 64)) * 2
    cols += list(range(2048, 2560)) + list(range(4352, 4608))
    sh = {}
    sh["w_inA"] = np.ascontiguousarray(w_in[:, :, cols])
    sh["w_inG"] = np.ascontiguousarray(w_in[:, :, 4608:7680])
    sh["poolw"] = np.ascontiguousarray(f("pool_w").transpose(0, 2, 1, 3))
    sh["wbp"], sh["wbh"], sh["wba"] = f("w_br_pool"), f("w_br_hg"), f("w_br_att")
    sh["wout"], sh["wup"], sh["wdn"] = f("w_out"), f("w_up"), f("w_down")
    sh["wpg"], sh["wpl"] = f("w_ple_gate"), f("w_ple")
    gv = np.zeros((3 * DEPTH + 1, D), np.float32)
    for L in range(DEPTH):
        gv[3 * L], gv[3 * L + 1], gv[3 * L + 2] = f("g_mix")[L], f("g_ffn")[L], f("g_ple")[L]
    gv[3 * DEPTH] = f("g_final")
    sh["gv"] = gv
    pp = np.zeros((128, 8 * DEPTH + DEPTH * PL), np.float32)
    pp[:, 0:4 * DEPTH] = f("lb_raw_f").reshape(DEPTH, 4, 128).transpose(2, 0, 1).reshape(128, 4 * DEPTH)
    pp[:, 4 * DEPTH:8 * DEPTH] = f("lb_raw_b").reshape(DEPTH, 4, 128).transpose(2, 0, 1).reshape(128, 4 * DEPTH)
    p64 = np.arange(128) % 64
    for L in range(DEPTH):
        b = 8 * DEPTH + L * PL
        pp[:, b] = f("hg_onorm")[L]
        pp[:, b + 1] = f("g_q")[L][p64]
        pp[:, b + 2] = f("g_k")[L][p64]
        pp[:, b + 3:b + 7] = f("pool_scale")[L].reshape(4, 128).T
        cw = f("conv_w")[L].reshape(3, NFC, 128)
        pp[:, b + 7:b + 7 + 3 * NFC] = cw.transpose(2, 0, 1).reshape(128, 3 * NFC)
        pp[:, b + 7 + 3 * NFC:b + 7 + 4 * NFC] = f("conv_b")[L].reshape(NFC, 128).T
    sh["pp"] = pp
    return sh


def _core_map(sh, xs, ps, NS, DEPTH):
    n = xs.shape[0]
    x = np.zeros((NS, D), np.float32)
    x[:n] = xs
    pT = np.zeros((DEPTH, 256, NS), np.float32)
    pT[:, :, :n] = ps.transpose(0, 2, 1)
    cst, maskc, rc, cosT, sinT = _consts(NS, n)
    m = dict(sh)
    m.update(x=x, pT=pT, cst=cst, maskc=maskc, rc=rc, cosT=cosT, sinT=sinT)
    return m


_NC_CACHE = {}


def run_seqs(inp, seqs, NS, DEPTH, dbg=False):
    key = (NS, DEPTH, dbg)
    if key not in _NC_CACHE:
        _NC_CACHE[key] = Builder(NS, DEPTH, dbg).build()
    nc = _NC_CACHE[key]
    sh = _shared(inp, DEPTH)
    in_maps = [_core_map(sh, np.asarray(xs, np.float32), np.asarray(ps, np.float32), NS, DEPTH) for xs, ps in seqs]
    res = run_bass_kernel_spmd(nc, in_maps, core_ids=list(range(len(seqs))))
    return res


def kernel(**inp):
    xp, xsm = np.asarray(inp["x_prompt"]), np.asarray(inp["x_sample"])
    pp_, ps_ = np.asarray(inp["p_prompt"]), np.asarray(inp["p_sample"])
    DEPTH = pp_.shape[0]
    NS = max(xp.shape[1], xsm.shape[1])
    seqs = [(xp[i], pp_[:, i]) for i in range(xp.shape[0])] + [(xsm[i], ps_[:, i]) for i in range(xsm.shape[0])]
    res = run_seqs(inp, seqs, NS, DEPTH)
    nb = xp.shape[0]
    yp = np.stack([res.results[i]["y"][:xp.shape[1]] for i in range(nb)]).astype(np.float32)
    ys = np.stack([res.results[nb + i]["y"][:xsm.shape[1]] for i in range(xsm.shape[0])]).astype(np.float32)
    return (yp, ys)
```

```python
import numpy as np
from contextlib import ExitStack
import concourse.bass as bass
import concourse.mybir as mybir
from concourse.bass_utils import run_bass_kernel_spmd

F32, BF16 = mybir.dt.float32, mybir.dt.bfloat16
AF = mybir.ActivationFunctionType
ALU = mybir.AluOpType

D = 1024
EPS = 1e-6
DFF = 2816
NFC = 2 * DFF // 128
PL = 1 + 1 + 1 + 4 + 3 * NFC + NFC
SAME_ENGINE_SYNC = True


class Op:
    __slots__ = ("eng", "fn", "deps", "isdma", "needed", "tok", "back")

    def __init__(self, eng, fn, isdma):
        self.eng, self.fn, self.isdma = eng, fn, isdma
        self.deps = set()
        self.needed = False
        self.tok = None
        self.back = None


class Prog:
    K = 8

    def __init__(self, nc):
        self.nc = nc
        self.ops = []
        self.last_w = {}
        self.readers = {}
        self.last_op = {}
        self.dma_recent = {"sp": [], "pool": [], "act": []}

    def op(self, eng, fn, r=(), w=(), dma=False):
        o = Op(eng, fn, dma)
        for x in r:
            lw = self.last_w.get(x)
            if lw is not None:
                o.deps.add(lw)
        for x in w:
            lw = self.last_w.get(x)
            if lw is not None:
                o.deps.add(lw)
            for rd in self.readers.get(x, ()):
                o.deps.add(rd)
        for x in r:
            self.readers.setdefault(x, []).append(o)
        for x in w:
            self.last_w[x] = o
            self.readers[x] = []
        o.deps.discard(o)
        self.ops.append(o)
        if dma:
            lst = self.dma_recent[eng]
            lst.append(o)
            if len(lst) > self.K:
                lst.pop(0)
        else:
            self.last_op[eng] = o
        return o

    def barrier(self):
        deps = set(self.last_op.values())
        for lst in self.dma_recent.values():
            deps.update(lst)
        for eng in ("pe", "act", "dve", "pool", "sp"):
            o = Op(eng, None, False)
            o.deps = set(deps)
            self.ops.append(o)
        self.last_w.clear()
        self.readers.clear()

    def emit(self):
        nc = self.nc
        for o in self.ops:
            for d in o.deps:
                d.needed = True
        cnt = {}
        dcnt = {}
        for o in self.ops:
            if o.isdma:
                k = dcnt.get(o.eng, 0)
                o.tok = (("d", o.eng, k % self.K), 16 * (k // self.K + 1))
                o.back = (("d", o.eng, k % self.K), 16 * (k // self.K)) if k >= self.K else None
                dcnt[o.eng] = k + 1
            elif o.fn is not None and o.needed:
                cnt[o.eng] = cnt.get(o.eng, 0) + 1
                o.tok = (("c", o.eng), cnt[o.eng])
        handles = {"pe": nc.tensor, "act": nc.scalar, "dve": nc.vector, "pool": nc.gpsimd, "sp": nc.sync}
        with ExitStack() as es:
            sems = {}
            for e in ("pe", "act", "dve", "pool"):
                sems[("c", e)] = es.enter_context(nc.semaphore("c_" + e))
            for e in dcnt:
                for k in range(self.K):
                    sems[("d", e, k)] = es.enter_context(nc.semaphore("d_%s%d" % (e, k)))
            block = es.enter_context(nc.Block())
            by_eng = {e: [o for o in self.ops if o.eng == e] for e in handles}

            def run(ename, h):
                waited = {}
                for o in by_eng[ename]:
                    need = {}
                    for d in o.deps:
                        if d.tok is None:
                            continue
                        if d.eng == ename and not d.isdma and not o.isdma:
                            if ename == "pe" or not SAME_ENGINE_SYNC:
                                continue
                        s, v = d.tok
                        if need.get(s, 0) < v:
                            need[s] = v
                    if o.back is not None:
                        s, v = o.back
                        if need.get(s, 0) < v:
                            need[s] = v
                    for s, v in need.items():
                        if waited.get(s, 0) < v:
                            h.wait_ge(sems[s], v)
                            waited[s] = v
                    if o.fn is None:
                        continue
                    inst = o.fn(h)
                    if o.tok is not None:
                        inst.then_inc(sems[o.tok[0]], 16 if o.isdma else 1)

            @block.tensor
            def _(h):
                run("pe", h)

            @block.scalar
            def _(h):
                run("act", h)

            @block.vector
            def _(h):
                run("dve", h)

            @block.gpsimd
            def _(h):
                run("pool", h)

            @block.sync
            def _(h):
                run("sp", h)


def run_pipe(gens):
    gens = list(gens)
    active = []
    i = 0
    while i < len(gens) or active:
        if i < len(gens):
            active.insert(0, gens[i])
            i += 1
        nxt = []
        for g in active:
            try:
                next(g)
                nxt.append(g)
            except StopIteration:
                pass
        active = nxt


class Rot:
    def __init__(self, tiles, name):
        self.tiles, self.name, self.i = tiles, name, 0

    def get(self):
        i = self.i % len(self.tiles)
        self.i += 1
        return self.tiles[i], "%s%d" % (self.name, i)


class Builder:
    def __init__(self, NS, DEPTH, dbg=False):
        self.NS, self.DEPTH, self.dbg = NS, DEPTH, dbg
        self.NT, self.NG, self.NCH = NS // 512, NS // 128, NS // 64
        self.nc = bass.Bass("TRN2", target_bir_lowering=False)
        self.p = Prog(self.nc)
        self.bk = 0
        self.only = None

    def din(self, name, shape, dt=F32):
        return self.nc.dram_tensor(name, list(shape), dt, kind="ExternalInput").ap()

    def dsc(self, name, shape, dt):
        return self.nc.dram_tensor(name, list(shape), dt, kind="ExternalOutput" if self.dbg else "Internal").ap()

    def sb(self, es, name, shape, dt):
        self.uid = getattr(self, "uid", 0) + 1
        return es.enter_context(self.nc.sbuf_tensor("%s_u%d" % (name, self.uid), list(shape), dt))

    def rot(self, es, name, shape, dt, n):
        return Rot([self.sb(es, "%s_%d" % (name, i), shape, dt) for i in range(n)], name)

    def bank(self, lo=0, hi=8):
        n = hi - lo
        b = lo + (self.bk % n)
        self.bk += 1
        return b, "ps%d" % b

    def psb(self, b, w=512):
        return self.PS[:, b * 512:b * 512 + w]

    def mm(self, out, lhsT, rhs, start, stop, r, w):
        return self.p.op("pe", lambda e: e.matmul(out, lhsT, rhs, start=start, stop=stop), r, w)

    def tr(self, out, in_, ident, r, w):
        return self.p.op("pe", lambda e: e.transpose(out, in_, ident), r, w)

    def act(self, out, in_, func, r, w, bias=None, scale=None, accum=None):
        kw = {}
        if bias is not None:
            kw["bias"] = bias
        if scale is not None:
            kw["scale"] = scale
        if accum is not None:
            kw["accum_out"] = accum
        return self.p.op("act", lambda e: e.activation(out, in_, func, **kw), r, w)

    def tt(self, out, a, b, op, r, w, eng="dve"):
        return self.p.op(eng, lambda e: e.tensor_tensor(out, a, b, op), r, w)

    def ts(self, out, a, s1, s2, op0, op1, r, w, eng="dve"):
        return self.p.op(eng, lambda e: e.tensor_scalar(out, a, s1, s2, op0, op1), r, w)

    def stt(self, out, a, s, b, op0, op1, r, w, eng="dve"):
        return self.p.op(eng, lambda e: e.scalar_tensor_tensor(out, a, s, b, op0, op1), r, w)

    def cp(self, out, in_, r, w, eng="dve"):
        if eng == "act":
            return self.p.op("act", lambda e: e.copy(out, in_), r, w)
        return self.p.op(eng, lambda e: e.tensor_copy(out, in_), r, w)

    def ms(self, ap, val, w, eng="dve"):
        return self.p.op(eng, lambda e: e.memset(ap, val), (), w)

    def dma(self, out, in_, r, w, q="sp", **kw):
        return self.p.op(q, lambda e: e.dma_start(out=out, in_=in_, **kw), r, w, dma=True)

    def wload(self, dst, src3, key, nk):
        for k in range(nk):
            self.dma(dst[:, k, :], src3[:, k, :], (), [key], q="pool", max_dma_last_dim=8192)

    def ppc(self, col):
        return self.PP[:, col:col + 1]

    def lbase(self, L):
        return 2 * self.DEPTH * 4 + L * PL

    def norm_rows(self, x_ap, xkey, gbc, gkey, maskcol, h_ap, hkey):
        junk, jk = self.r_junk.get()
        ss, sk = self.r_ss.get()
        self.act(junk[:], x_ap, AF.Square, [xkey], [jk, sk], accum=ss[:, 0:1])
        self.act(ss[:, 1:2], ss[:, 0:1], AF.Ln, [sk], [sk], bias=self.epsc[:, 0:1], scale=1.0 / D)
        self.act(ss[:, 2:3], ss[:, 1:2], AF.Exp, [sk], [sk], scale=-0.5)
        rs = ss[:, 2:3]
        if maskcol is not None:
            self.tt(ss[:, 3:4], ss[:, 2:3], maskcol, ALU.mult, [sk, "mask"], [sk])
            rs = ss[:, 3:4]
        self.stt(h_ap, x_ap, rs, gbc, ALU.mult, ALU.mult, [xkey, sk, gkey], [hkey])

    def transpose_rows(self, h_ap, hkey, dst3, dkey, col0, evac="act"):
        b, bkey = self.bank()
        psv = self.psb(b).bitcast(BF16)
        for k in range(8):
            self.tr(psv[:, k * 128:(k + 1) * 128], h_ap[:, k * 128:(k + 1) * 128], self.identb[:], [hkey, "cst"], [bkey])
        self.cp(dst3[:, :, col0:col0 + 128], psv.rearrange("p (k n) -> p k n", k=8), [bkey], [dkey], eng=evac)

    def build(self):
        nc, NS, DEPTH = self.nc, self.NS, self.DEPTH
        NT, NG, NCH = self.NT, self.NG, self.NCH
        self.x = self.din("x", [NS, D])
        self.pT = self.din("pT", [DEPTH, 256, NS])
        self.maskc = self.din("maskc", [128, NG])
        self.rc = self.din("rc", [4, NS])
        self.cosT = self.din("cosT", [128, NS])
        self.sinT = self.din("sinT", [128, NS])
        self.cst = self.din("cst", [128, 1280])
        self.gv = self.din("gv", [3 * DEPTH + 1, D])
        self.ppd = self.din("pp", [128, 8 * DEPTH + DEPTH * PL])
        self.w_inA = self.din("w_inA", [DEPTH, D, 4864])
        self.w_inG = self.din("w_inG", [DEPTH, D, 3072])
        self.poolw = self.din("poolw", [DEPTH, 128, 4, 128])
        self.wbp = self.din("wbp", [DEPTH, 512, D])
        self.wbh = self.din("wbh", [DEPTH, 512, D])
        self.wba = self.din("wba", [DEPTH, D, D])
        self.wout = self.din("wout", [DEPTH, D, D])
        self.wup = self.din("wup", [DEPTH, D, 2 * DFF])
        self.wdn = self.din("wdn", [DEPTH, DFF, D])
        self.wpg = self.din("wpg", [DEPTH, D, D])
        self.wpl = self.din("wpl", [DEPTH, 256, D])
        self.y = self.nc.dram_tensor("y", [NS, D], F32, kind="ExternalOutput").ap()
        self.XS = self.dsc("XS", [NS, D], F32)
        self.H1T = self.dsc("H1T", [D, NS], BF16)
        self.UT = self.dsc("UT", [512, NS + 16], BF16)
        self.QK = {(d, k): self.dsc("QK%s%s" % (d, k), [512, NS], BF16) for d in "fb" for k in "qk"}
        self.SG = self.dsc("SG", [512, NS], BF16)
        self.VH = self.dsc("VH", [NS, 512], BF16)
        self.QT = self.dsc("QT", [D, NS], BF16)
        self.KT2 = self.dsc("KT2", [512, NS], BF16)
        self.VA = self.dsc("VA", [NS, 512], BF16)
        self.OO = {d: self.dsc("OO" + d, [512, NS], F32) for d in "fb"}
        self.YA = self.dsc("YA", [D, NS], BF16)
        self.H2T = self.dsc("H2T", [D, NS + 2], BF16)
        self.MG = self.dsc("MG", [D, NS], BF16)

        with ExitStack() as es:
            self.PS = es.enter_context(nc.psum_tensor("PS", [128, 4096], F32))
            self.CST = self.sb(es, "CST", [128, 1280], F32)
            self.PP = self.sb(es, "PP", [128, 8 * DEPTH + DEPTH * PL], F32)
            self.LB = self.sb(es, "LB", [128, 2, 2, DEPTH * 4], F32)
            self.identb = self.sb(es, "identb", [128, 128], BF16)
            self.BDb = self.sb(es, "BDb", [128, 128], BF16)
            self.ONESb = self.sb(es, "ONESb", [128, 128], BF16)
            self.ROTb = self.sb(es, "ROTb", [128, 128], BF16)
            self.epsc = self.sb(es, "epsc", [128, 1], F32)
            self.maskt = self.sb(es, "maskt", [128, NG], F32)
            self.r_junk = self.rot(es, "junk", [128, D], BF16, 1)
            self.r_ss = self.rot(es, "ss", [128, 4], F32, 4)
            self.setup()
            self.phase_0()
            for L in range(DEPTH):
                on = lambda n: (self.only is None) or (n in self.only)
                with ExitStack() as es2:
                    self.SC = self.sb(es2, "SC", [128, 2, 3, 4, NCH], F32)
                    if on("A"):
                        self.phase_A(L)
                    if on("B"):
                        self.phase_B(L)
                if on("C1"):
                    self.phase_C1(L)
                if on("C2a"):
                    self.phase_C2a(L)
                if on("C2b"):
                    self.phase_C2b(L)
                if on("D1"):
                    self.phase_D1(L)
                if on("D2"):
                    self.phase_D2(L)
            self.p.barrier()
            self.p.emit()
        return nc

    def setup(self):
        DEPTH = self.DEPTH
        self.dma(self.CST[:], self.cst[:, :], (), ["cstf"])
        self.dma(self.PP[:], self.ppd[:, :], (), ["pp"])
        self.dma(self.maskt[:], self.maskc[:, :], (), ["mask"])
        for dst, c0 in ((self.identb, 0), (self.BDb, 128), (self.ONESb, 256), (self.ROTb, 384)):
            self.cp(dst[:], self.CST[:, c0:c0 + 128], ["cstf"], ["cst"])
        self.ms(self.epsc[:], EPS, ["cst"])
        with ExitStack() as es:
            e = self.sb(es, "lbe", [128, 2, DEPTH * 4], F32)
            s = self.sb(es, "lbs", [128, 2, 4], F32)
            self.act(e[:], self.PP[:, 0:8 * DEPTH].rearrange("p (a n) -> p a n", a=2), AF.Exp, ["pp"], ["lbe"])
            self.cp(s[:], e[:, :, 0:4], ["lbe"], ["lbs"])
            for d in range(1, DEPTH):
                self.tt(s[:], s[:], e[:, :, d * 4:(d + 1) * 4], ALU.add, ["lbs", "lbe"], ["lbs"])
            self.p.op("dve", lambda en: en.reciprocal(s[:], s[:]), ["lbs"], ["lbs"])
            for d in range(DEPTH):
                self.tt(e[:, :, d * 4:(d + 1) * 4], e[:, :, d * 4:(d + 1) * 4], s[:], ALU.mult, ["lbs", "lbe"], ["lbe"])
            self.ms(self.LB[:, :, 0, 0:4], 0.0, ["LB"])
            for d in range(1, DEPTH):
                self.tt(self.LB[:, :, 0, d * 4:(d + 1) * 4], self.LB[:, :, 0, (d - 1) * 4:d * 4],
                        e[:, :, d * 4:(d + 1) * 4], ALU.add, ["LB", "lbe"], ["LB"])
            self.ts(self.LB[:, :, 1, :], self.LB[:, :, 0, :], -1.0, 1.0, ALU.mult, ALU.add, ["LB"], ["LB"])
            for L in range(DEPTH):
                c = self.lbase(L) + 1
                self.ts(self.PP[:, c:c + 1], self.PP[:, c:c + 1], 0.125, None, ALU.mult, ALU.bypass, ["pp"], ["pp"])
            z = self.sb(es, "zpad", [128, 8, 8], BF16)
            self.ms(z[:], 0.0, ["zpad"])
            UTv = self.UT.rearrange("(k p) n -> p k n", p=128)
            H2v = self.H2T.rearrange("(k p) n -> p k n", p=128)
            NS = self.NS
            self.dma(UTv[:, :, 0:8], z[:, 0:4, :], ["zpad"], [])
            self.dma(UTv[:, :, NS + 8:NS + 16], z[:, 0:4, :], ["zpad"], [])
            self.dma(H2v[:, :, 0:1], z[:, :, 0:1], ["zpad"], [], allow_slow_non_contiguous=True)
            self.dma(H2v[:, :, NS + 1:NS + 2], z[:, :, 0:1], ["zpad"], [], allow_slow_non_contiguous=True)
            self.p.barrier()

    def load_g(self, tile, key, row):
        self.dma(tile[:], self.gv[row:row + 1, :].partition_broadcast(128), (), [key])

    def phase_0(self):
        with ExitStack() as es:
            g0 = self.sb(es, "g0", [128, D], F32)
            self.load_g(g0, "g0", 0)
            xr = self.rot(es, "p0x", [128, D], F32, 3)
            hr = self.rot(es, "p0h", [128, D], BF16, 2)
            hTr = self.rot(es, "p0hT", [128, 8, 512], BF16, 2)
            H1v = self.H1T.rearrange("(k p) n -> p k n", p=128)
            for t in range(self.NT):
                hT, hTk = hTr.get()
                for s in range(4):
                    g = t * 4 + s
                    xt, xk = xr.get()
                    self.dma(xt[:], self.x[g * 128:(g + 1) * 128, :], (), [xk])
                    self.dma(self.XS[g * 128:(g + 1) * 128, :], xt[:], [xk], [])
                    h, hk = hr.get()
                    self.norm_rows(xt[:], xk, g0[:], "g0", self.maskt[:, g:g + 1], h[:], hk)
                    self.transpose_rows(h[:], hk, hT, hTk, s * 128)
                self.dma(H1v[:, :, t * 512:(t + 1) * 512], hT[:], [hTk], [])
            self.p.barrier()

    def phase_A(self, L):
        NT = self.NT
        lb = self.lbase(L)
        with ExitStack() as es:
            WA = self.sb(es, "WA", [128, 8, 4864], BF16)
            self.wload(WA, self.w_inA[L].rearrange("(k p) f -> p k f", p=128), "WA", 8)
            hTr = self.rot(es, "a_hT", [128, 8, 512], BF16, 2)
            csr = self.rot(es, "a_cs", [128, 2, 512], F32, 1)
            uTr = self.rot(es, "a_uT", [128, 4, 512], BF16, 1)
            qkr = {k: self.rot(es, "a_qk" + k[0] + k[1], [128, 4, 512], BF16, 1) for k in self.QK}
            sgr = self.rot(es, "a_sg", [128, 4, 512], BF16, 1)
            qtr = self.rot(es, "a_qt", [128, 8, 512], BF16, 1)
            ktr = self.rot(es, "a_kt", [128, 4, 512], BF16, 1)
            vhr = self.rot(es, "a_vh", [128, 4, 512], BF16, 1)
            var = self.rot(es, "a_va", [128, 4, 512], BF16, 1)
            qsr = self.rot(es, "a_qs", [128, 512], F32, 2)
            f32r = self.rot(es, "a_f", [128, 512], F32, 10)
            b16r = self.rot(es, "a_b", [128, 512], BF16, 8)
            smr = self.rot(es, "a_sm", [128, 8], F32, 4)
            H1v = self.H1T.rearrange("(k p) n -> p k n", p=128)
            reset = self.CST[:, 768:1280]

            def proj(hT, hTk, col0):
                b, bk = self.bank()
                ps = self.psb(b)
                for k in range(8):
                    self.mm(ps, WA[:, k, col0:col0 + 128], hT[:, k, :], k == 0, k == 7, ["WA", hTk], [bk])
                return ps, bk

            for t in range(NT):
                c0 = t * 512
                hT, hTk = hTr.get()
                self.dma(hT[:], H1v[:, :, c0:c0 + 512], (), [hTk])
                cs, csk = csr.get()
                self.dma(cs[:, 0, :], self.cosT[:, c0:c0 + 512], (), [csk])
                self.dma(cs[:, 1, :], self.sinT[:, c0:c0 + 512], (), [csk])
                uT, uTk = uTr.get()
                for g in range(4):
                    ps, bk = proj(hT, hTk, g * 128)
                    self.cp(uT[:, g, :], ps, [bk], [uTk], eng="act")
                self.dma(self.UT.rearrange("(k p) n -> p k n", p=128)[:, :, 8 + c0:8 + c0 + 512], uT[:], [uTk], [])
                st = {k: qkr[k].get() for k in self.QK}
                sg, sgk = sgr.get()
                for h in range(4):
                    ps, bk = proj(hT, hTk, 512 + h * 128)
                    qs, qsk = qsr.get()
                    self.act(qs[:], ps, AF.Copy, [bk], [qsk], scale=128.0 ** -0.5)
                    for di, d in enumerate("fb"):
                        ps, bk = proj(hT, hTk, 1024 + di * 512 + h * 128)
                        f, fk = f32r.get()
                        self.act(f[:], ps, AF.Sigmoid, [bk], [fk])
                        lbc = self.LB[:, di, 0, L * 4 + h:L * 4 + h + 1]
                        omc = self.LB[:, di, 1, L * 4 + h:L * 4 + h + 1]
                        self.ts(f[:], f[:], omc, lbc, ALU.mult, ALU.add, [fk, "LB"], [fk])
                        lf, lfk = f32r.get()
                        self.act(lf[:], f[:], AF.Ln, [fk], [lfk])
                        kk, kkk = f32r.get()
                        self.ts(kk[:], f[:], -1.0, 1.0, ALU.mult, ALU.add, [fk], [kkk])
                        cum, cumk = f32r.get()
                        self.p.op("dve", lambda e, cum=cum, lf=lf: e.tensor_tensor_scan(
                            cum[:], reset, lf[:], 0.0, ALU.mult, ALU.add), [lfk, "cstf"], [cumk])
                        A, Ak = f32r.get()
                        sm, smk = smr.get()
                        c3 = lambda tl: tl[:].rearrange("p (n j) -> p n j", j=64)
                        Lc = cum[:, 63:512:64]
                        scd = lambda j: self.SC[:, di, j, h, t * 8:(t + 1) * 8]
                        sck = "SC"
                        if d == "f":
                            m = cum[:, 31:512:64]
                            self.tt(c3(A), c3(cum), m.unsqueeze(2).to_broadcast([128, 8, 64]), ALU.subtract, [cumk], [Ak])
                            self.tt(sm[:], Lc, m, ALU.subtract, [cumk], [smk])
                            self.act(scd(0), m, AF.Exp, [cumk], [sck])
                            self.act(scd(2), sm[:], AF.Exp, [smk], [sck])
                        else:
                            E, Ek = f32r.get()
                            self.tt(E[:], cum[:], lf[:], ALU.subtract, [cumk, lfk], [Ek])
                            m = E[:, 31:512:64]
                            self.tt(c3(A), m.unsqueeze(2).to_broadcast([128, 8, 64]), c3(E), ALU.subtract, [Ek], [Ak])
                            self.tt(sm[:], Lc, m, ALU.subtract, [cumk, Ek], [smk])
                            self.act(scd(0), sm[:], AF.Exp, [smk], [sck])
                            self.act(scd(2), m, AF.Exp, [Ek], [sck])
                        self.act(scd(1), Lc, AF.Exp, [cumk], [sck])
                        eA, eAk = f32r.get()
                        self.act(eA[:], A[:], AF.Exp, [Ak], [eAk])
                        qt_, qtk_ = st[(d, "q")]
                        self.tt(qt_[:, h, :], qs[:], eA[:], ALU.mult, [qsk, eAk], [qtk_])
                        eN, eNk = f32r.get()
                        self.act(eN[:], A[:], AF.Exp, [Ak], [eNk], scale=-1.0)
                        kt_, ktk_ = st[(d, "k")]
                        self.tt(kt_[:, h, :], kk[:], eN[:], ALU.mult, [kkk, eNk], [ktk_])
                    ps, bk = proj(hT, hTk, 2048 + h * 128)
                    self.act(sg[:, h, :], ps, AF.Silu, [bk], [sgk])
                for k in self.QK:
                    tl, tk = st[k]
                    self.dma(self.QK[k].rearrange("(h p) n -> p h n", p=128)[:, :, c0:c0 + 512], tl[:], [tk], [])
                self.dma(self.SG.rearrange("(h p) n -> p h n", p=128)[:, :, c0:c0 + 512], sg[:], [sgk], [])
                qt, qtk = qtr.get()
                kt, ktk = ktr.get()
                def qk_thread(oc):
                    ps, bk = proj(hT, hTk, 2560 + oc * 128)
                    gcol = self.ppc(lb + 1) if oc < 8 else self.ppc(lb + 2)
                    sq, sqk = b16r.get()
                    self.act(sq[:], ps, AF.Square, [bk], [sqk])
                    yield
                    b2, bk2 = self.bank()
                    self.mm(self.psb(b2), self.BDb[:], sq[:], True, True, ["cst", sqk], [bk2])
                    r1, r1k = f32r.get()
                    self.act(r1[:], self.psb(b2), AF.Ln, [bk2], [r1k], bias=self.epsc[:, 0:1], scale=1.0 / 64)
                    self.act(r1[:], r1[:], AF.Exp, [r1k], [r1k], scale=-0.5)
                    zn, znk = b16r.get()
                    self.stt(zn[:], ps, gcol, r1[:], ALU.mult, ALU.mult, [bk, "pp", r1k], [znk])
                    yield
                    b3, bk3 = self.bank()
                    self.mm(self.psb(b3), self.ROTb[:], zn[:], True, True, ["cst", znk], [bk3])
                    t2, t2k = f32r.get()
                    self.tt(t2[:], zn[:], cs[:, 0, :], ALU.mult, [znk, csk], [t2k])
                    t3, t3k = f32r.get()
                    self.tt(t3[:], self.psb(b3), cs[:, 1, :], ALU.mult, [bk3, csk], [t3k])
                    if oc < 8:
                        self.tt(qt[:, oc, :], t2[:], t3[:], ALU.add, [t2k, t3k], [qtk], eng="pool")
                    else:
                        self.tt(kt[:, oc - 8, :], t2[:], t3[:], ALU.add, [t2k, t3k], [ktk], eng="pool")
                    yield

                run_pipe(qk_thread(oc) for oc in range(12))
                self.dma(self.QT.rearrange("(k p) n -> p k n", p=128)[:, :, c0:c0 + 512], qt[:], [qtk], [])
                self.dma(self.KT2.rearrange("(k p) n -> p k n", p=128)[:, :, c0:c0 + 512], kt[:], [ktk], [])
                vh, vhk = vhr.get()
                va, vak = var.get()
                for s in range(4):
                    g = t * 4 + s
                    b, bk = self.bank()
                    for k in range(8):
                        self.mm(self.psb(b), hT[:, k, s * 128:(s + 1) * 128], WA[:, k, 4096:4608], k == 0, k == 7, ["WA", hTk], [bk])
                    self.cp(vh[:, s, :], self.psb(b), [bk], [vhk], eng="act")
                    b, bk = self.bank()
                    for k in range(8):
                        self.mm(self.psb(b, 256), hT[:, k, s * 128:(s + 1) * 128], WA[:, k, 4608:4864], k == 0, k == 7, ["WA", hTk], [bk])
                    vav = va[:, s, :].rearrange("p (h c) -> p h c", c=128)
                    self.cp(vav[:, :, 0:64], self.psb(b, 256).rearrange("p (h c) -> p h c", c=64), [bk], [vak])
                    self.cp(vav[:, :, 64:128], self.maskt[:, g:g + 1].unsqueeze(1).to_broadcast([128, 4, 64]), ["mask"], [vak])
                self.dma(self.VH.rearrange("(s p) f -> p s f", p=128)[:, t * 4:(t + 1) * 4, :], vh[:], [vhk], [])
                self.dma(self.VA.rearrange("(s p) f -> p s f", p=128)[:, t * 4:(t + 1) * 4, :], va[:], [vak], [])
            self.p.barrier()

    def phase_B(self, L):
        NT = self.NT
        with ExitStack() as es:
            Sf = self.sb(es, "b_Sf", [128, 2, 4, 128], F32)
            Sb = self.sb(es, "b_Sb", [128, 2, 4, 128], BF16)
            self.ms(Sf[:], 0.0, ["Sf00", "Sf01", "Sf02", "Sf03", "Sf10", "Sf11", "Sf12", "Sf13"])
            self.ms(Sb[:], 0.0, ["Sb00", "Sb01", "Sb02", "Sb03", "Sb10", "Sb11", "Sb12", "Sb13"])
            qr = {d: self.rot(es, "b_q" + d, [128, 4, 512], BF16, 2) for d in "fb"}
            kr = {d: self.rot(es, "b_k" + d, [128, 4, 512], BF16, 2) for d in "fb"}
            vr = {d: self.rot(es, "b_v" + d, [128, 4, 512], BF16, 2) for d in "fb"}
            ktr = {d: self.rot(es, "b_kt" + d, [128, 4, 4, 128], BF16, 2) for d in "fb"}
            otr = {d: self.rot(es, "b_o" + d, [128, 4, 512], F32, 2) for d in "fb"}
            smr = self.rot(es, "b_sm", [128, 128], BF16, 8)
            tmr = self.rot(es, "b_tm", [128, 128], F32, 8)
            MK = {"f": self.CST[:, 512:640], "b": self.CST[:, 640:768]}
            for step in range(NT):
                tiles = {"f": step, "b": NT - 1 - step}
                cur = {}
                for di, d in enumerate("fb"):
                    t = tiles[d]
                    c0 = t * 512
                    q, qk = qr[d].get()
                    k, kk = kr[d].get()
                    v, vk = vr[d].get()
                    self.dma(q[:], self.QK[(d, "q")].rearrange("(h p) n -> p h n", p=128)[:, :, c0:c0 + 512], (), [qk])
                    self.dma(k[:], self.QK[(d, "k")].rearrange("(h p) n -> p h n", p=128)[:, :, c0:c0 + 512], (), [kk])
                    self.dma(v[:], self.VH.rearrange("(s p) f -> p s f", p=128)[:, t * 4:(t + 1) * 4, :], (), [vk])
                    ktk_t, ktk = ktr[d].get()
                    for h in range(4):
                        b, bk = self.bank(4, 8)
                        psv = self.psb(b).bitcast(BF16)
                        for bl in range(4):
                            self.tr(psv[:, bl * 128:(bl + 1) * 128], k[:, h, bl * 128:(bl + 1) * 128], self.identb[:], [kk, "cst"], [bk])
                        self.cp(ktk_t[:, h, :, :], psv[:, 0:512].rearrange("p (b n) -> p b n", b=4), [bk], [ktk], eng="act")
                    o, ok = otr[d].get()
                    cur[d] = (t, q, qk, k, kk, v, vk, ktk_t, ktk, o, ok)
                for bi in range(4):
                    for di, d in enumerate("fb"):
                        t, q, qk, k, kk, v, vk, ktk_t, ktk, o, ok = cur[d]
                        blk = bi if d == "f" else 3 - bi
                        order = (0, 1) if d == "f" else (1, 0)
                        bc = slice(blk * 128, (blk + 1) * 128)
                        sms = []
                        for h in range(4):
                            b, bk = self.bank(4, 8)
                            ps = self.psb(b, 128)
                            self.mm(ps, k[:, h, bc], q[:, h, bc], True, True, [kk, qk], [bk])
                            sm, smk = smr.get()
                            self.tt(sm[:], ps, MK[d], ALU.mult, [bk, "cstf"], [smk])
                            sms.append((sm, smk))

                        def upd(h, ca):
                            gch = t * 8 + blk * 2 + ca
                            sfk, sbk = "Sf%d%d" % (di, h), "Sb%d%d" % (di, h)
                            b2, bk2 = self.bank(4, 8)
                            rows = slice(ca * 64, (ca + 1) * 64)
                            self.mm(self.psb(b2, 128), ktk_t[rows, h, blk, :], v[rows, blk, h * 128:(h + 1) * 128], True, True, [ktk, vk], [bk2])
                            tm, tmk = tmr.get()
                            self.ts(tm[:], Sf[:, di, h, :], self.SC[:, di, 1, h, gch:gch + 1], None, ALU.mult, ALU.bypass, [sfk, "SC"], [tmk])
                            self.stt(Sf[:, di, h, :], self.psb(b2, 128), self.SC[:, di, 2, h, gch:gch + 1], tm[:], ALU.mult, ALU.add, [bk2, "SC", tmk], [sfk])
                            nxt = gch + 1 if d == "f" else gch - 1
                            if 0 <= nxt < self.NCH:
                                self.ts(Sb[:, di, h, :], Sf[:, di, h, :], self.SC[:, di, 0, h, nxt:nxt + 1], None, ALU.mult, ALU.bypass, [sfk, "SC"], [sbk])

                        for h in range(4):
                            sm, smk = sms[h]
                            okey = "ps%d" % h
                            O = self.psb(h, 128)
                            ca = order[0]
                            self.mm(O, v[:, blk, h * 128:(h + 1) * 128], sm[:], True, False, [vk, smk], [okey])
                            self.mm(O[:, ca * 64:(ca + 1) * 64], Sb[:, di, h, :], q[:, h, blk * 128 + ca * 64:blk * 128 + (ca + 1) * 64],
                                    False, False, ["Sb%d%d" % (di, h), qk], [okey])
                            upd(h, ca)
                        for h in range(4):
                            okey = "ps%d" % h
                            O = self.psb(h, 128)
                            cb = order[1]
                            self.mm(O[:, cb * 64:(cb + 1) * 64], Sb[:, di, h, :], q[:, h, blk * 128 + cb * 64:blk * 128 + (cb + 1) * 64],
                                    False, True, ["Sb%d%d" % (di, h), qk], [okey])
                            upd(h, cb)
                            self.cp(o[:, h, bc], O, [okey], [ok], eng="act")
                for d in "fb":
                    t, q, qk, k, kk, v, vk, ktk_t, ktk, o, ok = cur[d]
                    self.dma(self.OO[d].rearrange("(h p) n -> p h n", p=128)[:, :, t * 512:(t + 1) * 512], o[:], [ok], [])
            self.p.barrier()

    def phase_C1(self, L):
        NS, NT, NG = self.NS, self.NT, self.NG
        with ExitStack() as es:
            KT = self.sb(es, "c_KT", [128, 4, NS], BF16)
            for k in range(4):
                self.dma(KT[:, k, :], self.KT2.rearrange("(k p) n -> p k n", p=128)[:, k, :], (), ["KT"])
            VAt = self.sb(es, "c_VA", [128, NG, 512], BF16)
            VAv = self.VA.rearrange("(g p) f -> p g f", p=128)
            for g0 in range(0, NG, 16):
                g1 = min(NG, g0 + 16)
                self.dma(VAt[:, g0:g1, :], VAv[:, g0:g1, :], (), ["VAt"])
            qr = self.rot(es, "c_q", [128, 8, 512], BF16, 2)
            ptr = self.rot(es, "c_pt", [128, 1024], BF16, 3)
            yar = self.rot(es, "c_ya", [64, 16, 512], BF16, 2)
            rdr = self.rot(es, "c_rd", [128, 512], F32, 2)
            QTv = self.QT.rearrange("(k p) n -> p k n", p=128)
            YAv = self.YA.rearrange("(h d) n -> d h n", d=64)
            qtiles = {}

            def load_q(qt):
                q, qk = qr.get()
                self.dma(q[:], QTv[:, :, qt * 512:(qt + 1) * 512], (), [qk])
                qtiles[qt] = (q, qk)

            load_q(0)
            steps = [(qt, c, kt) for qt in range(NT) for c in range(8) for kt in range(NG)]
            pend = None
            ya = yak = None
            for i in range(len(steps) + 1):
                if i < len(steps):
                    qt, c, kt = steps[i]
                    if c == 0 and kt == 0:
                        if qt + 1 < NT:
                            load_q(qt + 1)
                        ya, yak = yar.get()
                    q, qk = qtiles[qt]
                    kvh = c // 2
                    s0 = (i % 2) * 2
                    kc = slice(kt * 128, (kt + 1) * 128)
                    self.mm(self.psb(s0), KT[0:64, kvh, kc], q[0:64, c, :], True, True, ["KT", qk], ["ps%d" % s0])
                    self.mm(self.psb(s0 + 1), KT[64:128, kvh, kc], q[64:128, c, :], True, True, ["KT", qk], ["ps%d" % (s0 + 1)])
                    pt, ptk = ptr.get()
                    self.act(pt[:], self.PS[:, s0 * 512:s0 * 512 + 1024], AF.Exp, ["ps%d" % s0, "ps%d" % (s0 + 1)], [ptk])
                    cur = (qt, c, kt, pt, ptk, ya, yak)
                if pend is not None:
                    pqt, pc, pkt, ppt, pptk, pya, pyak = pend
                    pkvh = pc // 2
                    ob0 = 4 + (pc % 2) * 2
                    lhs = VAt[:, pkt, pkvh * 128:(pkvh + 1) * 128]
                    for j in range(2):
                        self.mm(self.psb(ob0 + j), lhs, ppt[:, j * 512:(j + 1) * 512], pkt == 0, pkt == NG - 1,
                                ["VAt", pptk], ["ps%d" % (ob0 + j)])
                    if pkt == NG - 1:
                        for j in range(2):
                            h = 2 * pc + j
                            ob = ob0 + j
                            okey = "ps%d" % ob
                            rd, rdk = rdr.get()
                            self.p.op("dve", lambda e, rd=rd, ob=ob: e.reciprocal(rd[64:128, :], self.PS[64:128, ob * 512:(ob + 1) * 512]), [okey], [rdk])
                            self.tt(pya[:, h, :], self.PS[0:64, ob * 512:(ob + 1) * 512], rd[64:128, :], ALU.mult, [okey, rdk], [pyak])
                        if pc == 7:
                            self.dma(YAv[:, :, pqt * 512:(pqt + 1) * 512], pya[:], [pyak], [])
                pend = cur if i < len(steps) else None
            self.p.barrier()

    def phase_C2a(self, L):
        NT = self.NT
        lb = self.lbase(L)
        with ExitStack() as es:
            WG = self.sb(es, "e_WG", [128, 8, 3072], BF16)
            self.wload(WG, self.w_inG[L].rearrange("(k p) f -> p k f", p=128), "WG", 8)
            WP = self.sb(es, "e_WP", [128, 4, 128], BF16)
            self.dma(WP[:], self.poolw[L], (), ["WP"], q="pool")
            WBP = self.sb(es, "e_WBP", [128, 4, D], BF16)
            self.wload(WBP, self.wbp[L].rearrange("(k p) f -> p k f", p=128), "WBP", 4)
            WBH = self.sb(es, "e_WBH", [128, 4, D], BF16)
            self.wload(WBH, self.wbh[L].rearrange("(k p) f -> p k f", p=128), "WBH", 4)
            WBA = self.sb(es, "e_WBA", [128, 8, D], BF16)
            self.wload(WBA, self.wba[L].rearrange("(k p) f -> p k f", p=128), "WBA", 8)
            hTr = self.rot(es, "e_hT", [128, 8, 512], BF16, 1)
            ur = self.rot(es, "e_u", [128, 4, 528], BF16, 1)
            oor = self.rot(es, "e_oo", [128, 512], F32, 4)
            sgr = self.rot(es, "e_sg", [128, 4, 512], BF16, 1)
            yar = self.rot(es, "e_ya", [128, 8, 512], BF16, 1)
            rcr = self.rot(es, "e_rc", [128, 4, 512], F32, 1)
            ypr = self.rot(es, "e_yp", [128, 4, 512], BF16, 1)
            yhr = self.rot(es, "e_yh", [128, 4, 512], BF16, 1)
            mgr = self.rot(es, "e_mg", [128, 8, 512], BF16, 1)
            gtr = self.rot(es, "e_gt", [128, 512], F32, 3)
            f32r = self.rot(es, "e_f", [128, 528], F32, 6)
            b16r = self.rot(es, "e_b", [128, 512], BF16, 2)
            for t in range(NT):
                c0 = t * 512
                hT, hTk = hTr.get()
                self.dma(hT[:], self.H1T.rearrange("(k p) n -> p k n", p=128)[:, :, c0:c0 + 512], (), [hTk])
                u, uk = ur.get()
                self.dma(u[:], self.UT.rearrange("(k p) n -> p k n", p=128)[:, :, c0:c0 + 528], (), [uk])
                sg, sgk = sgr.get()
                self.dma(sg[:], self.SG.rearrange("(h p) n -> p h n", p=128)[:, :, c0:c0 + 512], (), [sgk])
                ya, yak = yar.get()
                self.dma(ya[:], self.YA.rearrange("(k p) n -> p k n", p=128)[:, :, c0:c0 + 512], (), [yak])
                rct, rck = rcr.get()
                for g in range(4):
                    self.dma(rct[:, g, :], self.rc[g:g + 1, c0:c0 + 512].partition_broadcast(128), (), [rck])
                yp, ypk = ypr.get()
                for g in range(4):
                    S, Sk = f32r.get()
                    self.tt(S[:, 1:528], u[:, g, 0:527], u[:, g, 1:528], ALU.add, [uk], [Sk], eng="pool")
                    lo, hi = 1, 528
                    for lv in range(g):
                        sh = 1 << lv
                        S2, S2k = f32r.get()
                        nlo, nhi = lo + sh, hi - sh
                        self.tt(S2[:, nlo:nhi], S[:, nlo - sh:nhi - sh], S[:, nlo + sh:nhi + sh], ALU.add, [Sk], [S2k], eng="pool")
                        S, Sk, lo, hi = S2, S2k, nlo, nhi
                    m1, m1k = f32r.get()
                    self.tt(m1[:, 0:512], S[:, 8:520], rct[:, g, :], ALU.mult, [Sk, rck], [m1k])
                    mx, mxk = b16r.get()
                    self.tt(mx[:], m1[:, 0:512], u[:, g, 8:520], ALU.subtract, [m1k, uk], [mxk])
                    b, bk = self.bank()
                    self.mm(self.psb(b), WP[:, g, :], mx[:], True, True, ["WP", mxk], [bk])
                    self.act(yp[:, g, :], self.psb(b), AF.Identity, [bk, "pp"], [ypk], scale=self.ppc(lb + 3 + g), bias=0.0)
                yh, yhk = yhr.get()
                for h in range(4):
                    of, ofk = oor.get()
                    ob, obk = oor.get()
                    self.dma(of[:], self.OO["f"][h * 128:(h + 1) * 128, c0:c0 + 512], (), [ofk])
                    self.dma(ob[:], self.OO["b"][h * 128:(h + 1) * 128, c0:c0 + 512], (), [obk])
                    self.tt(of[:], of[:], ob[:], ALU.add, [ofk, obk], [ofk])
                    sq, sqk = b16r.get()
                    self.act(sq[:], of[:], AF.Square, [ofk], [sqk])
                    b, bk = self.bank()
                    self.mm(self.psb(b), self.ONESb[:], sq[:], True, True, ["cst", sqk], [bk])
                    r1, r1k = f32r.get()
                    self.act(r1[:, 0:512], self.psb(b), AF.Ln, [bk], [r1k], bias=self.epsc[:, 0:1], scale=1.0 / 128)
                    self.act(r1[:, 0:512], r1[:, 0:512], AF.Exp, [r1k], [r1k], scale=-0.5)
                    self.stt(ob[:], of[:], self.ppc(lb), r1[:, 0:512], ALU.mult, ALU.mult, [ofk, "pp", r1k], [obk])
                    self.tt(yh[:, h, :], ob[:], sg[:, h, :], ALU.mult, [obk, sgk], [yhk])
                mg, mgk = mgr.get()
                for oc in range(8):
                    ocs = slice(oc * 128, (oc + 1) * 128)
                    bp, bpk = self.bank()
                    for k in range(4):
                        self.mm(self.psb(bp), WBP[:, k, ocs], yp[:, k, :], k == 0, k == 3, ["WBP", ypk], [bpk])
                    bh, bhk = self.bank()
                    for k in range(4):
                        self.mm(self.psb(bh), WBH[:, k, ocs], yh[:, k, :], k == 0, k == 3, ["WBH", yhk], [bhk])
                    ba, bak = self.bank()
                    for k in range(8):
                        self.mm(self.psb(ba), WBA[:, k, ocs], ya[:, k, :], k == 0, k == 7, ["WBA", yak], [bak])
                    ms = []
                    for j, (bb, bbk) in enumerate(((bp, bpk), (bh, bhk), (ba, bak))):
                        bg, bgk = self.bank()
                        for k in range(8):
                            self.mm(self.psb(bg), WG[:, k, j * D + oc * 128:j * D + (oc + 1) * 128], hT[:, k, :], k == 0, k == 7, ["WG", hTk], [bgk])
                        gt, gtk = gtr.get()
                        self.act(gt[:], self.psb(bg), AF.Sigmoid, [bgk], [gtk])
                        self.tt(gt[:], gt[:], self.psb(bb), ALU.mult, [gtk, bbk], [gtk])
                        ms.append((gt, gtk))
                    self.tt(ms[0][0][:], ms[0][0][:], ms[1][0][:], ALU.add, [ms[0][1], ms[1][1]], [ms[0][1]], eng="pool")
                    self.tt(mg[:, oc, :], ms[0][0][:], ms[2][0][:], ALU.add, [ms[0][1], ms[2][1]], [mgk], eng="pool")
                self.dma(self.MG.rearrange("(k p) n -> p k n", p=128)[:, :, c0:c0 + 512], mg[:], [mgk], [])
            self.p.barrier()

    def phase_C2b(self, L):
        NT = self.NT
        with ExitStack() as es:
            WO = self.sb(es, "f_WO", [128, 8, D], BF16)
            self.wload(WO, self.wout[L].rearrange("(k p) f -> p k f", p=128), "WO", 8)
            gf = self.sb(es, "f_g", [128, D], F32)
            self.load_g(gf, "gf", 3 * L + 1)
            mgr = self.rot(es, "f_mg", [128, 8, 512], BF16, 2)
            xr = self.rot(es, "f_x", [128, D], F32, 4)
            hr = self.rot(es, "f_h", [128, D], BF16, 3)
            hTr = self.rot(es, "f_hT", [128, 8, 512], BF16, 2)
            tiles = {}

            def grp(g):
                t, s_ = g // 4, g % 4
                c0 = t * 512
                if s_ == 0:
                    mg, mgk = mgr.get()
                    self.dma(mg[:], self.MG.rearrange("(k p) n -> p k n", p=128)[:, :, c0:c0 + 512], (), [mgk])
                    tiles[t] = (mg, mgk) + hTr.get()
                mg, mgk, hT, hTk = tiles[t]
                x, xk = xr.get()
                self.dma(x[:], self.XS[g * 128:(g + 1) * 128, :], (), [xk])
                for fh in range(2):
                    b, bk = self.bank()
                    for k in range(8):
                        self.mm(self.psb(b), mg[:, k, s_ * 128:(s_ + 1) * 128], WO[:, k, fh * 512:(fh + 1) * 512], k == 0, k == 7, [mgk, "WO"], [bk])
                    self.tt(x[:, fh * 512:(fh + 1) * 512], x[:, fh * 512:(fh + 1) * 512], self.psb(b), ALU.add, [xk, bk], [xk])
                self.dma(self.XS[g * 128:(g + 1) * 128, :], x[:], [xk], [])
                h, hk = hr.get()
                self.norm_rows(x[:], xk, gf[:], "gf", self.maskt[:, g:g + 1], h[:], hk)
                yield
                self.transpose_rows(h[:], hk, hT, hTk, s_ * 128)
                if s_ == 3:
                    self.dma(self.H2T.rearrange("(k p) n -> p k n", p=128)[:, :, 1 + c0:1 + c0 + 512], hT[:], [hTk], [])
                yield

            run_pipe(grp(g) for g in range(self.NG))
            self.p.barrier()

    def phase_D1(self, L):
        NS = self.NS
        lb = self.lbase(L)
        cw = lb + 7
        with ExitStack() as es:
            WU = self.sb(es, "g_WU", [128, 8, 2 * DFF], BF16)
            self.wload(WU, self.wup[L].rearrange("(k p) f -> p k f", p=128), "WU", 8)
            WD = self.sb(es, "g_WD", [128, 22, D], BF16)
            self.wload(WD, self.wdn[L].rearrange("(k p) f -> p k f", p=128), "WD", 22)
            hTr = self.rot(es, "g_hT", [128, 8, 384], BF16, 2)
            acr = self.rot(es, "g_ac", [128, 22, 382], BF16, 2)
            xr = self.rot(es, "g_x", [128, D], F32, 1)
            f32r = self.rot(es, "g_f", [128, 382], F32, 9)
            H2v = self.H2T.rearrange("(k p) n -> p k n", p=128)

            def conv(ps, bk, ci, w):
                t1, t1k = f32r.get()
                self.act(t1[:, 0:w], ps[:, 1:w + 1], AF.Identity, [bk, "pp"], [t1k], scale=self.ppc(cw + NFC + ci), bias=self.ppc(cw + 3 * NFC + ci))
                self.stt(t1[:, 0:w], ps[:, 0:w], self.ppc(cw + ci), t1[:, 0:w], ALU.mult, ALU.add, [bk, "pp", t1k], [t1k])
                self.stt(t1[:, 0:w], ps[:, 2:w + 2], self.ppc(cw + 2 * NFC + ci), t1[:, 0:w], ALU.mult, ALU.add, [bk, "pp", t1k], [t1k])
                return t1, t1k

            def tile(c0, w):
                hT, hTk = hTr.get()
                self.dma(hT[:, :, 0:w + 2], H2v[:, :, c0:c0 + w + 2], (), [hTk])
                ac, ack = acr.get()

                def pair(j):
                    res = []
                    for ci in (j, 22 + j):
                        b, bk = self.bank()
                        ps = self.psb(b, w + 2)
                        for k in range(8):
                            self.mm(ps, WU[:, k, ci * 128:(ci + 1) * 128], hT[:, k, 0:w + 2], k == 0, k == 7, ["WU", hTk], [bk])
                        t1, t1k = f32r.get()
                        self.act(t1[:, 0:w], ps[:, 1:w + 1], AF.Identity, [bk, "pp"], [t1k], scale=self.ppc(cw + NFC + ci), bias=self.ppc(cw + 3 * NFC + ci))
                        res.append((ps, bk, ci, t1, t1k))
                    yield
                    for ps, bk, ci, t1, t1k in res:
                        self.stt(t1[:, 0:w], ps[:, 0:w], self.ppc(cw + ci), t1[:, 0:w], ALU.mult, ALU.add, [bk, "pp", t1k], [t1k])
                        self.stt(t1[:, 0:w], ps[:, 2:w + 2], self.ppc(cw + 2 * NFC + ci), t1[:, 0:w], ALU.mult, ALU.add, [bk, "pp", t1k], [t1k])
                    yield
                    (_, _, _, a, ak), (_, _, _, gg, ggk) = res
                    self.act(a[:, 0:w], a[:, 0:w], AF.Gelu_apprx_tanh, [ak], [ak])
                    yield
                    self.tt(ac[:, j, 0:w], a[:, 0:w], gg[:, 0:w], ALU.mult, [ak, ggk], [ack])
                    yield

                run_pipe(pair(j) for j in range(22))
                yield
                for s0 in range(0, w, 128):
                    m = min(128, w - s0)
                    r0 = c0 + s0
                    x, xk = xr.get()
                    self.dma(x[0:m, :], self.XS[r0:r0 + m, :], (), [xk])
                    for fh in range(2):
                        b, bk = self.bank()
                        for k in range(22):
                            self.mm(self.PS[0:m, b * 512:(b + 1) * 512], ac[:, k, s0:s0 + m], WD[:, k, fh * 512:(fh + 1) * 512], k == 0, k == 21, [ack, "WD"], [bk])
                        self.tt(x[0:m, fh * 512:(fh + 1) * 512], x[0:m, fh * 512:(fh + 1) * 512], self.PS[0:m, b * 512:(b + 1) * 512], ALU.add, [xk, bk], [xk])
                    self.dma(self.XS[r0:r0 + m, :], x[0:m, :], [xk], [])
                yield

            starts = list(range(0, NS, 382))
            run_pipe(tile(c0, min(382, NS - c0)) for c0 in starts)
            self.p.barrier()

    def phase_D2(self, L):
        NT = self.NT
        last = L == self.DEPTH - 1
        with ExitStack() as es:
            WPG = self.sb(es, "h_WPG", [128, 8, D], BF16)
            self.wload(WPG, self.wpg[L].rearrange("(k p) f -> p k f", p=128), "WPG", 8)
            WPL = self.sb(es, "h_WPL", [128, 2, D], BF16)
            self.wload(WPL, self.wpl[L].rearrange("(k p) f -> p k f", p=128), "WPL", 2)
            gp = self.sb(es, "h_gp", [128, D], F32)
            self.load_g(gp, "gp", 3 * L + 2)
            gn = self.sb(es, "h_gn", [128, D], F32)
            self.load_g(gn, "gn", 3 * self.DEPTH if last else 3 * (L + 1))
            ptr = self.rot(es, "h_pT", [128, 2, 512], BF16, 2)
            xr = self.rot(es, "h_x", [128, D], F32, 5)
            hr = self.rot(es, "h_h", [128, D], BF16, 6)
            h3r = self.rot(es, "h_h3T", [128, 8, 128], BF16, 3)
            hTr = self.rot(es, "h_hT", [128, 8, 512], BF16, 2)
            yr = self.rot(es, "h_y", [128, D], F32, 2)
            gsr = self.rot(es, "h_gs", [128, 512], F32, 4)
            tiles = {}

            def grp(g):
                t, s_ = g // 4, g % 4
                c0 = t * 512
                if s_ == 0:
                    pt, ptk = ptr.get()
                    self.dma(pt[:], self.pT[L].rearrange("(k p) n -> p k n", p=128)[:, :, c0:c0 + 512], (), [ptk], q="pool")
                    tiles[t] = (pt, ptk) + hTr.get()
                pt, ptk, hT, hTk = tiles[t]
                x, xk = xr.get()
                self.dma(x[:], self.XS[g * 128:(g + 1) * 128, :], (), [xk])
                h, hk = hr.get()
                self.norm_rows(x[:], xk, gp[:], "gp", None, h[:], hk)
                yield
                h3, h3k = h3r.get()
                self.transpose_rows(h[:], hk, h3, h3k, 0)
                yield
                for fh in range(2):
                    fs = slice(fh * 512, (fh + 1) * 512)
                    b, bk = self.bank()
                    for k in range(8):
                        self.mm(self.psb(b), h3[:, k, :], WPG[:, k, fs], k == 0, k == 7, [h3k, "WPG"], [bk])
                    gs, gsk = gsr.get()
                    self.act(gs[:], self.psb(b), AF.Sigmoid, [bk], [gsk])
                    b2, bk2 = self.bank()
                    for k in range(2):
                        self.mm(self.psb(b2), pt[:, k, s_ * 128:(s_ + 1) * 128], WPL[:, k, fs], k == 0, k == 1, [ptk, "WPL"], [bk2])
                    self.tt(gs[:], gs[:], self.psb(b2), ALU.mult, [gsk, bk2], [gsk])
                    self.tt(x[:, fs], x[:, fs], gs[:], ALU.add, [xk, gsk], [xk])
                if last:
                    yt, ytk = yr.get()
                    self.norm_rows(x[:], xk, gn[:], "gn", None, yt[:], ytk)
                    self.dma(self.y[g * 128:(g + 1) * 128, :], yt[:], [ytk], [])
                    yield
                else:
                    self.dma(self.XS[g * 128:(g + 1) * 128, :], x[:], [xk], [])
                    h2, h2k = hr.get()
                    self.norm_rows(x[:], xk, gn[:], "gn", self.maskt[:, g:g + 1], h2[:], h2k)
                    yield
                    self.transpose_rows(h2[:], h2k, hT, hTk, s_ * 128)
                    if s_ == 3:
                        self.dma(self.H1T.rearrange("(k p) n -> p k n", p=128)[:, :, c0:c0 + 512], hT[:], [hTk], [])
                    yield

            run_pipe(grp(g) for g in range(self.NG))
            self.p.barrier()


def _consts(NS, n_real):
    cst = np.zeros((128, 1280), np.float32)
    idx = np.arange(128)
    cst[:, 0:128] = np.eye(128, dtype=np.float32)
    cst[:, 128:256] = (idx[:, None] // 64 == idx[None, :] // 64)
    cst[:, 256:384] = 1.0
    rot = np.zeros((128, 128), np.float32)
    for m in range(128):
        if (m % 32) < 16:
            rot[m + 16, m] = -1.0
        else:
            rot[m - 16, m] = 1.0
    cst[:, 384:512] = rot
    same = idx[:, None] // 64 == idx[None, :] // 64
    cst[:, 512:640] = same & (idx[:, None] <= idx[None, :])
    cst[:, 640:768] = same & (idx[:, None] >= idx[None, :])
    cst[:, 768:1280] = (np.arange(512) % 64 != 0)[None, :]
    t = np.arange(NS)
    maskc = (t < n_real).astype(np.float32).reshape(NS // 128, 128).T.copy()
    rc = np.ones((4, NS), np.float32)
    for g, w in enumerate((2, 4, 8, 16)):
        lo = np.clip(t - w // 2, 0, n_real)
        hi = np.clip(t + w // 2, 0, n_real)
        c = np.maximum(hi - lo, 1).astype(np.float32)
        rc[g] = np.where(t < n_real, np.float32(1.0) / c, np.float32(1.0))
    inv_freq = (1.0 / (np.float32(10000.0) ** (np.arange(0, 32, 2, dtype=np.float32) / np.float32(32)))).astype(np.float32)
    d = idx % 64
    a = d // 32
    i = d % 16
    pos = np.where(a[:, None] == 0, (t // 64)[None, :], (t % 64)[None, :]).astype(np.float32)
    ang = pos * inv_freq[i][:, None]
    return cst, maskc, rc, np.cos(ang).astype(np.float32), np.sin(ang).astype(np.float32)


def _shared(inp, DEPTH):
    f = lambda k: np.asarray(inp[k], np.float32)
    w_in = f("w_in")
    cols = list(range(0, 2048)) + list(range(2560, 3072)) + list(range(3072, 4096))
    for kvh in range(4):
        cols += list(range(4096 + kvh * 64, 4096 + (kvh + 1) * 64)) * 2
    cols += list(range(2048, 2560)) + list(range(4352, 4608))
    sh = {}
    sh["w_inA"] = np.ascontiguousarray(w_in[:, :, cols])
    sh["w_inG"] = np.ascontiguousarray(w_in[:, :, 4608:7680])
    sh["poolw"] = np.ascontiguousarray(f("pool_w").transpose(0, 2, 1, 3))
    sh["wbp"], sh["wbh"], sh["wba"] = f("w_br_pool"), f("w_br_hg"), f("w_br_att")
    sh["wout"], sh["wup"], sh["wdn"] = f("w_out"), f("w_up"), f("w_down")
    sh["wpg"], sh["wpl"] = f("w_ple_gate"), f("w_ple")
    gv = np.zeros((3 * DEPTH + 1, D), np.float32)
    for L in range(DEPTH):
        gv[3 * L], gv[3 * L + 1], gv[3 * L + 2] = f("g_mix")[L], f("g_ffn")[L], f("g_ple")[L]
    gv[3 * DEPTH] = f("g_final")
    sh["gv"] = gv
    pp = np.zeros((128, 8 * DEPTH + DEPTH * PL), np.float32)
    pp[:, 0:4 * DEPTH] = f("lb_raw_f").reshape(DEPTH, 4, 128).transpose(2, 0, 1).reshape(128, 4 * DEPTH)
    pp[:, 4 * DEPTH:8 * DEPTH] = f("lb_raw_b").reshape(DEPTH, 4, 128).transpose(2, 0, 1).reshape(128, 4 * DEPTH)
    p64 = np.arange(128) % 64
    for L in range(DEPTH):
        b = 8 * DEPTH + L * PL
        pp[:, b] = f("hg_onorm")[L]
        pp[:, b + 1] = f("g_q")[L][p64]
        pp[:, b + 2] = f("g_k")[L][p64]
        pp[:, b + 3:b + 7] = f("pool_scale")[L].reshape(4, 128).T
        cw = f("conv_w")[L].reshape(3, NFC, 128)
        pp[:, b + 7:b + 7 + 3 * NFC] = cw.transpose(2, 0, 1).reshape(128, 3 * NFC)
        pp[:, b + 7 + 3 * NFC:b + 7 + 4 * NFC] = f("conv_b")[L].reshape(NFC, 128).T
    sh["pp"] = pp
    return sh


def _core_map(sh, xs, ps, NS, DEPTH):
    n = xs.shape[0]
    x = np.zeros((NS, D), np.float32)
    x[:n] = xs
    pT = np.zeros((DEPTH, 256, NS), np.float32)
    pT[:, :, :n] = ps.transpose(0, 2, 1)
    cst, maskc, rc, cosT, sinT = _consts(NS, n)
    m = dict(sh)
    m.update(x=x, pT=pT, cst=cst, maskc=maskc, rc=rc, cosT=cosT, sinT=sinT)
    return m


_NC_CACHE = {}


def run_seqs(inp, seqs, NS, DEPTH, dbg=False):
    key = (NS, DEPTH, dbg)
    if key not in _NC_CACHE:
        _NC_CACHE[key] = Builder(NS, DEPTH, dbg).build()
    nc = _NC_CACHE[key]
    sh = _shared(inp, DEPTH)
    in_maps = [_core_map(sh, np.asarray(xs, np.float32), np.asarray(ps, np.float32), NS, DEPTH) for xs, ps in seqs]
    res = run_bass_kernel_spmd(nc, in_maps, core_ids=list(range(len(seqs))))
    return res


def kernel(**inp):
    xp, xsm = np.asarray(inp["x_prompt"]), np.asarray(inp["x_sample"])
    pp_, ps_ = np.asarray(inp["p_prompt"]), np.asarray(inp["p_sample"])
    DEPTH = pp_.shape[0]
    NS = max(xp.shape[1], xsm.shape[1])
    seqs = [(xp[i], pp_[:, i]) for i in range(xp.shape[0])] + [(xsm[i], ps_[:, i]) for i in range(xsm.shape[0])]
    res = run_seqs(inp, seqs, NS, DEPTH)
    nb = xp.shape[0]
    yp = np.stack([res.results[i]["y"][:xp.shape[1]] for i in range(nb)]).astype(np.float32)
    ys = np.stack([res.results[nb + i]["y"][:xsm.shape[1]] for i in range(xsm.shape[0])]).astype(np.float32)
    return (yp, ys)
```

```python
import numpy as np
from contextlib import ExitStack
import concourse.bass as bass
import concourse.mybir as mybir
from concourse.bass_utils import run_bass_kernel_spmd

F32, BF16 = mybir.dt.float32, mybir.dt.bfloat16
AF = mybir.ActivationFunctionType
ALU = mybir.AluOpType

D = 1024
EPS = 1e-6
DFF = 2816
NFC = 2 * DFF // 128
PL = 1 + 1 + 1 + 4 + 3 * NFC + NFC
SAME_ENGINE_SYNC = True


class Op:
    __slots__ = ("eng", "fn", "deps", "isdma", "needed", "tok", "back")

    def __init__(self, eng, fn, isdma):
        self.eng, self.fn, self.isdma = eng, fn, isdma
        self.deps = set()
        self.needed = False
        self.tok = None
        self.back = None


class Prog:
    K = 8

    def __init__(self, nc):
        self.nc = nc
        self.ops = []
        self.last_w = {}
        self.readers = {}
        self.last_op = {}
        self.dma_recent = {"sp": [], "pool": [], "act": []}

    def op(self, eng, fn, r=(), w=(), dma=False):
        o = Op(eng, fn, dma)
        for x in r:
            lw = self.last_w.get(x)
            if lw is not None:
                o.deps.add(lw)
        for x in w:
            lw = self.last_w.get(x)
            if lw is not None:
                o.deps.add(lw)
            for rd in self.readers.get(x, ()):
                o.deps.add(rd)
        for x in r:
            self.readers.setdefault(x, []).append(o)
        for x in w:
            self.last_w[x] = o
            self.readers[x] = []
        o.deps.discard(o)
        self.ops.append(o)
        if dma:
            lst = self.dma_recent[eng]
            lst.append(o)
            if len(lst) > self.K:
                lst.pop(0)
        else:
            self.last_op[eng] = o
        return o

    def barrier(self):
        deps = set(self.last_op.values())
        for lst in self.dma_recent.values():
            deps.update(lst)
        for eng in ("pe", "act", "dve", "pool", "sp"):
            o = Op(eng, None, False)
            o.deps = set(deps)
            self.ops.append(o)
        self.last_w.clear()
        self.readers.clear()

    def emit(self):
        nc = self.nc
        for o in self.ops:
            for d in o.deps:
                d.needed = True
        cnt = {}
        dcnt = {}
        for o in self.ops:
            if o.isdma:
                k = dcnt.get(o.eng, 0)
                o.tok = (("d", o.eng, k % self.K), 16 * (k // self.K + 1))
                o.back = (("d", o.eng, k % self.K), 16 * (k // self.K)) if k >= self.K else None
                dcnt[o.eng] = k + 1
            elif o.fn is not None and o.needed:
                cnt[o.eng] = cnt.get(o.eng, 0) + 1
                o.tok = (("c", o.eng), cnt[o.eng])
        handles = {"pe": nc.tensor, "act": nc.scalar, "dve": nc.vector, "pool": nc.gpsimd, "sp": nc.sync}
        with ExitStack() as es:
            sems = {}
            for e in ("pe", "act", "dve", "pool"):
                sems[("c", e)] = es.enter_context(nc.semaphore("c_" + e))
            for e in dcnt:
                for k in range(self.K):
                    sems[("d", e, k)] = es.enter_context(nc.semaphore("d_%s%d" % (e, k)))
            block = es.enter_context(nc.Block())
            by_eng = {e: [o for o in self.ops if o.eng == e] for e in handles}

            def run(ename, h):
                waited = {}
                for o in by_eng[ename]:
                    need = {}
                    for d in o.deps:
                        if d.tok is None:
                            continue
                        if d.eng == ename and not d.isdma and not o.isdma:
                            if ename == "pe" or not SAME_ENGINE_SYNC:
                                continue
                        s, v = d.tok
                        if need.get(s, 0) < v:
                            need[s] = v
                    if o.back is not None:
                        s, v = o.back
                        if need.get(s, 0) < v:
                            need[s] = v
                    for s, v in need.items():
                        if waited.get(s, 0) < v:
                            h.wait_ge(sems[s], v)
                            waited[s] = v
                    if o.fn is None:
                        continue
                    inst = o.fn(h)
                    if o.tok is not None:
                        inst.then_inc(sems[o.tok[0]], 16 if o.isdma else 1)

            @block.tensor
            def _(h):
                run("pe", h)

            @block.scalar
            def _(h):
                run("act", h)

            @block.vector
            def _(h):
                run("dve", h)

            @block.gpsimd
            def _(h):
                run("pool", h)

            @block.sync
            def _(h):
                run("sp", h)


def run_pipe(gens):
    gens = list(gens)
    active = []
    i = 0
    while i < len(gens) or active:
        if i < len(gens):
            active.insert(0, gens[i])
            i += 1
        nxt = []
        for g in active:
            try:
                next(g)
                nxt.append(g)
            except StopIteration:
                pass
        active = nxt


class Rot:
    def __init__(self, tiles, name):
        self.tiles, self.name, self.i = tiles, name, 0

    def get(self):
        i = self.i % len(self.tiles)
        self.i += 1
        return self.tiles[i], "%s%d" % (self.name, i)


class Builder:
    def __init__(self, NS, DEPTH, dbg=False):
        self.NS, self.DEPTH, self.dbg = NS, DEPTH, dbg
        self.NT, self.NG, self.NCH = NS // 512, NS // 128, NS // 64
        self.nc = bass.Bass("TRN2", target_bir_lowering=False)
        self.p = Prog(self.nc)
        self.bk = 0
        self.only = None

    def din(self, name, shape, dt=F32):
        return self.nc.dram_tensor(name, list(shape), dt, kind="ExternalInput").ap()

    def dsc(self, name, shape, dt):
        return self.nc.dram_tensor(name, list(shape), dt, kind="ExternalOutput" if self.dbg else "Internal").ap()

    def sb(self, es, name, shape, dt):
        self.uid = getattr(self, "uid", 0) + 1
        return es.enter_context(self.nc.sbuf_tensor("%s_u%d" % (name, self.uid), list(shape), dt))

    def rot(self, es, name, shape, dt, n):
        return Rot([self.sb(es, "%s_%d" % (name, i), shape, dt) for i in range(n)], name)

    def bank(self, lo=0, hi=8):
        n = hi - lo
        b = lo + (self.bk % n)
        self.bk += 1
        return b, "ps%d" % b

    def psb(self, b, w=512):
        return self.PS[:, b * 512:b * 512 + w]

    def mm(self, out, lhsT, rhs, start, stop, r, w):
        return self.p.op("pe", lambda e: e.matmul(out, lhsT, rhs, start=start, stop=stop), r, w)

    def tr(self, out, in_, ident, r, w):
        return self.p.op("pe", lambda e: e.transpose(out, in_, ident), r, w)

    def act(self, out, in_, func, r, w, bias=None, scale=None, accum=None):
        kw = {}
        if bias is not None:
            kw["bias"] = bias
        if scale is not None:
            kw["scale"] = scale
        if accum is not None:
            kw["accum_out"] = accum
        return self.p.op("act", lambda e: e.activation(out, in_, func, **kw), r, w)

    def tt(self, out, a, b, op, r, w, eng="dve"):
        return self.p.op(eng, lambda e: e.tensor_tensor(out, a, b, op), r, w)

    def ts(self, out, a, s1, s2, op0, op1, r, w, eng="dve"):
        return self.p.op(eng, lambda e: e.tensor_scalar(out, a, s1, s2, op0, op1), r, w)

    def stt(self, out, a, s, b, op0, op1, r, w, eng="dve"):
        return self.p.op(eng, lambda e: e.scalar_tensor_tensor(out, a, s, b, op0, op1), r, w)

    def cp(self, out, in_, r, w, eng="dve"):
        if eng == "act":
            return self.p.op("act", lambda e: e.copy(out, in_), r, w)
        return self.p.op(eng, lambda e: e.tensor_copy(out, in_), r, w)

    def ms(self, ap, val, w, eng="dve"):
        return self.p.op(eng, lambda e: e.memset(ap, val), (), w)

    def dma(self, out, in_, r, w, q="sp", **kw):
        return self.p.op(q, lambda e: e.dma_start(out=out, in_=in_, **kw), r, w, dma=True)

    def wload(self, dst, src3, key, nk):
        for k in range(nk):
            self.dma(dst[:, k, :], src3[:, k, :], (), [key], q="pool", max_dma_last_dim=8192)

    def ppc(self, col):
        return self.PP[:, col:col + 1]

    def lbase(self, L):
        return 2 * self.DEPTH * 4 + L * PL

    def norm_rows(self, x_ap, xkey, gbc, gkey, maskcol, h_ap, hkey):
        junk, jk = self.r_junk.get()
        ss, sk = self.r_ss.get()
        self.act(junk[:], x_ap, AF.Square, [xkey], [jk, sk], accum=ss[:, 0:1])
        self.act(ss[:, 1:2], ss[:, 0:1], AF.Ln, [sk], [sk], bias=self.epsc[:, 0:1], scale=1.0 / D)
        self.act(ss[:, 2:3], ss[:, 1:2], AF.Exp, [sk], [sk], scale=-0.5)
        rs = ss[:, 2:3]
        if maskcol is not None:
            self.tt(ss[:, 3:4], ss[:, 2:3], maskcol, ALU.mult, [sk, "mask"], [sk])
            rs = ss[:, 3:4]
        self.stt(h_ap, x_ap, rs, gbc, ALU.mult, ALU.mult, [xkey, sk, gkey], [hkey])

    def transpose_rows(self, h_ap, hkey, dst3, dkey, col0, evac="act"):
        b, bkey = self.bank()
        psv = self.psb(b).bitcast(BF16)
        for k in range(8):
            self.tr(psv[:, k * 128:(k + 1) * 128], h_ap[:, k * 128:(k + 1) * 128], self.identb[:], [hkey, "cst"], [bkey])
        self.cp(dst3[:, :, col0:col0 + 128], psv.rearrange("p (k n) -> p k n", k=8), [bkey], [dkey], eng=evac)

    def build(self):
        nc, NS, DEPTH = self.nc, self.NS, self.DEPTH
        NT, NG, NCH = self.NT, self.NG, self.NCH
        self.x = self.din("x", [NS, D])
        self.pT = self.din("pT", [DEPTH, 256, NS])
        self.maskc = self.din("maskc", [128, NG])
        self.rc = self.din("rc", [4, NS])
        self.cosT = self.din("cosT", [128, NS])
        self.sinT = self.din("sinT", [128, NS])
        self.cst = self.din("cst", [128, 1280])
        self.gv = self.din("gv", [3 * DEPTH + 1, D])
        self.ppd = self.din("pp", [128, 8 * DEPTH + DEPTH * PL])
        self.w_inA = self.din("w_inA", [DEPTH, D, 4864])
        self.w_inG = self.din("w_inG", [DEPTH, D, 3072])
        self.poolw = self.din("poolw", [DEPTH, 128, 4, 128])
        self.wbp = self.din("wbp", [DEPTH, 512, D])
        self.wbh = self.din("wbh", [DEPTH, 512, D])
        self.wba = self.din("wba", [DEPTH, D, D])
        self.wout = self.din("wout", [DEPTH, D, D])
        self.wup = self.din("wup", [DEPTH, D, 2 * DFF])
        self.wdn = self.din("wdn", [DEPTH, DFF, D])
        self.wpg = self.din("wpg", [DEPTH, D, D])
        self.wpl = self.din("wpl", [DEPTH, 256, D])
        self.y = self.nc.dram_tensor("y", [NS, D], F32, kind="ExternalOutput").ap()
        self.XS = self.dsc("XS", [NS, D], F32)
        self.H1T = self.dsc("H1T", [D, NS], BF16)
        self.UT = self.dsc("UT", [512, NS + 16], BF16)
        self.QK = {(d, k): self.dsc("QK%s%s" % (d, k), [512, NS], BF16) for d in "fb" for k in "qk"}
        self.SG = self.dsc("SG", [512, NS], BF16)
        self.VH = self.dsc("VH", [NS, 512], BF16)
        self.QT = self.dsc("QT", [D, NS], BF16)
        self.KT2 = self.dsc("KT2", [512, NS], BF16)
        self.VA = self.dsc("VA", [NS, 512], BF16)
        self.OO = {d: self.dsc("OO" + d, [512, NS], F32) for d in "fb"}
        self.YA = self.dsc("YA", [D, NS], BF16)
        self.H2T = self.dsc("H2T", [D, NS + 2], BF16)
        self.MG = self.dsc("MG", [D, NS], BF16)

        with ExitStack() as es:
            self.PS = es.enter_context(nc.psum_tensor("PS", [128, 4096], F32))
            self.CST = self.sb(es, "CST", [128, 1280], F32)
            self.PP = self.sb(es, "PP", [128, 8 * DEPTH + DEPTH * PL], F32)
            self.LB = self.sb(es, "LB", [128, 2, 2, DEPTH * 4], F32)
            self.identb = self.sb(es, "identb", [128, 128], BF16)
            self.BDb = self.sb(es, "BDb", [128, 128], BF16)
            self.ONESb = self.sb(es, "ONESb", [128, 128], BF16)
            self.ROTb = self.sb(es, "ROTb", [128, 128], BF16)
            self.epsc = self.sb(es, "epsc", [128, 1], F32)
            self.maskt = self.sb(es, "maskt", [128, NG], F32)
            self.r_junk = self.rot(es, "junk", [128, D], BF16, 1)
            self.r_ss = self.rot(es, "ss", [128, 4], F32, 4)
            self.setup()
            self.phase_0()
            for L in range(DEPTH):
                on = lambda n: (self.only is None) or (n in self.only)
                with ExitStack() as es2:
                    self.SC = self.sb(es2, "SC", [128, 2, 3, 4, NCH], F32)
                    if on("A"):
                        self.phase_A(L)
                    if on("B"):
                        self.phase_B(L)
                if on("C1"):
                    self.phase_C1(L)
                if on("C2a"):
                    self.phase_C2a(L)
                if on("C2b"):
                    self.phase_C2b(L)
                if on("D1"):
                    self.phase_D1(L)
                if on("D2"):
                    self.phase_D2(L)
            self.p.barrier()
            self.p.emit()
        return nc

    def setup(self):
        DEPTH = self.DEPTH
        self.dma(self.CST[:], self.cst[:, :], (), ["cstf"])
        self.dma(self.PP[:], self.ppd[:, :], (), ["pp"])
        self.dma(self.maskt[:], self.maskc[:, :], (), ["mask"])
        for dst, c0 in ((self.identb, 0), (self.BDb, 128), (self.ONESb, 256), (self.ROTb, 384)):
            self.cp(dst[:], self.CST[:, c0:c0 + 128], ["cstf"], ["cst"])
        self.ms(self.epsc[:], EPS, ["cst"])
        with ExitStack() as es:
            e = self.sb(es, "lbe", [128, 2, DEPTH * 4], F32)
            s = self.sb(es, "lbs", [128, 2, 4], F32)
            self.act(e[:], self.PP[:, 0:8 * DEPTH].rearrange("p (a n) -> p a n", a=2), AF.Exp, ["pp"], ["lbe"])
            self.cp(s[:], e[:, :, 0:4], ["lbe"], ["lbs"])
            for d in range(1, DEPTH):
                self.tt(s[:], s[:], e[:, :, d * 4:(d + 1) * 4], ALU.add, ["lbs", "lbe"], ["lbs"])
            self.p.op("dve", lambda en: en.reciprocal(s[:], s[:]), ["lbs"], ["lbs"])
            for d in range(DEPTH):
                self.tt(e[:, :, d * 4:(d + 1) * 4], e[:, :, d * 4:(d + 1) * 4], s[:], ALU.mult, ["lbs", "lbe"], ["lbe"])
            self.ms(self.LB[:, :, 0, 0:4], 0.0, ["LB"])
            for d in range(1, DEPTH):
                self.tt(self.LB[:, :, 0, d * 4:(d + 1) * 4], self.LB[:, :, 0, (d - 1) * 4:d * 4],
                        e[:, :, d * 4:(d + 1) * 4], ALU.add, ["LB", "lbe"], ["LB"])
            self.ts(self.LB[:, :, 1, :], self.LB[:, :, 0, :], -1.0, 1.0, ALU.mult, ALU.add, ["LB"], ["LB"])
            for L in range(DEPTH):
                c = self.lbase(L) + 1
                self.ts(self.PP[:, c:c + 1], self.PP[:, c:c + 1], 0.125, None, ALU.mult, ALU.bypass, ["pp"], ["pp"])
            z = self.sb(es, "zpad", [128, 8, 8], BF16)
            self.ms(z[:], 0.0, ["zpad"])
            UTv = self.UT.rearrange("(k p) n -> p k n", p=128)
            H2v = self.H2T.rearrange("(k p) n -> p k n", p=128)
            NS = self.NS
            self.dma(UTv[:, :, 0:8], z[:, 0:4, :], ["zpad"], [])
            self.dma(UTv[:, :, NS + 8:NS + 16], z[:, 0:4, :], ["zpad"], [])
            self.dma(H2v[:, :, 0:1], z[:, :, 0:1], ["zpad"], [], allow_slow_non_contiguous=True)
            self.dma(H2v[:, :, NS + 1:NS + 2], z[:, :, 0:1], ["zpad"], [], allow_slow_non_contiguous=True)
            self.p.barrier()

    def load_g(self, tile, key, row):
        self.dma(tile[:], self.gv[row:row + 1, :].partition_broadcast(128), (), [key])

    def phase_0(self):
        with ExitStack() as es:
            g0 = self.sb(es, "g0", [128, D], F32)
            self.load_g(g0, "g0", 0)
            xr = self.rot(es, "p0x", [128, D], F32, 3)
            hr = self.rot(es, "p0h", [128, D], BF16, 2)
            hTr = self.rot(es, "p0hT", [128, 8, 512], BF16, 2)
            H1v = self.H1T.rearrange("(k p) n -> p k n", p=128)
            for t in range(self.NT):
                hT, hTk = hTr.get()
                for s in range(4):
                    g = t * 4 + s
                    xt, xk = xr.get()
                    self.dma(xt[:], self.x[g * 128:(g + 1) * 128, :], (), [xk])
                    self.dma(self.XS[g * 128:(g + 1) * 128, :], xt[:], [xk], [])
                    h, hk = hr.get()
                    self.norm_rows(xt[:], xk, g0[:], "g0", self.maskt[:, g:g + 1], h[:], hk)
                    self.transpose_rows(h[:], hk, hT, hTk, s * 128)
                self.dma(H1v[:, :, t * 512:(t + 1) * 512], hT[:], [hTk], [])
            self.p.barrier()

    def phase_A(self, L):
        NT = self.NT
        lb = self.lbase(L)
        with ExitStack() as es:
            WA = self.sb(es, "WA", [128, 8, 4864], BF16)
            self.wload(WA, self.w_inA[L].rearrange("(k p) f -> p k f", p=128), "WA", 8)
            hTr = self.rot(es, "a_hT", [128, 8, 512], BF16, 2)
            csr = self.rot(es, "a_cs", [128, 2, 512], F32, 1)
            uTr = self.rot(es, "a_uT", [128, 4, 512], BF16, 1)
            qkr = {k: self.rot(es, "a_qk" + k[0] + k[1], [128, 4, 512], BF16, 1) for k in self.QK}
            sgr = self.rot(es, "a_sg", [128, 4, 512], BF16, 1)
            qtr = self.rot(es, "a_qt", [128, 8, 512], BF16, 1)
            ktr = self.rot(es, "a_kt", [128, 4, 512], BF16, 1)
            vhr = self.rot(es, "a_vh", [128, 4, 512], BF16, 1)
            var = self.rot(es, "a_va", [128, 4, 512], BF16, 1)
            qsr = self.rot(es, "a_qs", [128, 512], F32, 2)
            f32r = self.rot(es, "a_f", [128, 512], F32, 10)
            b16r = self.rot(es, "a_b", [128, 512], BF16, 8)
            smr = self.rot(es, "a_sm", [128, 8], F32, 4)
            H1v = self.H1T.rearrange("(k p) n -> p k n", p=128)
            reset = self.CST[:, 768:1280]

            def proj(hT, hTk, col0):
                b, bk = self.bank()
                ps = self.psb(b)
                for k in range(8):
                    self.mm(ps, WA[:, k, col0:col0 + 128], hT[:, k, :], k == 0, k == 7, ["WA", hTk], [bk])
                return ps, bk

            for t in range(NT):
                c0 = t * 512
                hT, hTk = hTr.get()
                self.dma(hT[:], H1v[:, :, c0:c0 + 512], (), [hTk])
                cs, csk = csr.get()
                self.dma(cs[:, 0, :], self.cosT[:, c0:c0 + 512], (), [csk])
                self.dma(cs[:, 1, :], self.sinT[:, c0:c0 + 512], (), [csk])
                uT, uTk = uTr.get()
                for g in range(4):
                    ps, bk = proj(hT, hTk, g * 128)
                    self.cp(uT[:, g, :], ps, [bk], [uTk], eng="act")
                self.dma(self.UT.rearrange("(k p) n -> p k n", p=128)[:, :, 8 + c0:8 + c0 + 512], uT[:], [uTk], [])
                st = {k: qkr[k].get() for k in self.QK}
                sg, sgk = sgr.get()
                for h in range(4):
                    ps, bk = proj(hT, hTk, 512 + h * 128)
                    qs, qsk = qsr.get()
                    self.act(qs[:], ps, AF.Copy, [bk], [qsk], scale=128.0 ** -0.5)
                    for di, d in enumerate("fb"):
                        ps, bk = proj(hT, hTk, 1024 + di * 512 + h * 128)
                        f, fk = f32r.get()
                        self.act(f[:], ps, AF.Sigmoid, [bk], [fk])
                        lbc = self.LB[:, di, 0, L * 4 + h:L * 4 + h + 1]
                        omc = self.LB[:, di, 1, L * 4 + h:L * 4 + h + 1]
                        self.ts(f[:], f[:], omc, lbc, ALU.mult, ALU.add, [fk, "LB"], [fk])
                        lf, lfk = f32r.get()
                        self.act(lf[:], f[:], AF.Ln, [fk], [lfk])
                        kk, kkk = f32r.get()
                        self.ts(kk[:], f[:], -1.0, 1.0, ALU.mult, ALU.add, [fk], [kkk])
                        cum, cumk = f32r.get()
                        self.p.op("dve", lambda e, cum=cum, lf=lf: e.tensor_tensor_scan(
                            cum[:], reset, lf[:], 0.0, ALU.mult, ALU.add), [lfk, "cstf"], [cumk])
                        A, Ak = f32r.get()
                        sm, smk = smr.get()
                        c3 = lambda tl: tl[:].rearrange("p (n j) -> p n j", j=64)
                        Lc = cum[:, 63:512:64]
                        scd = lambda j: self.SC[:, di, j, h, t * 8:(t + 1) * 8]
                        sck = "SC"
                        if d == "f":
                            m = cum[:, 31:512:64]
                            self.tt(c3(A), c3(cum), m.unsqueeze(2).to_broadcast([128, 8, 64]), ALU.subtract, [cumk], [Ak])
                            self.tt(sm[:], Lc, m, ALU.subtract, [cumk], [smk])
                            self.act(scd(0), m, AF.Exp, [cumk], [sck])
                            self.act(scd(2), sm[:], AF.Exp, [smk], [sck])
                        else:
                            E, Ek = f32r.get()
                            self.tt(E[:], cum[:], lf[:], ALU.subtract, [cumk, lfk], [Ek])
                            m = E[:, 31:512:64]
                            self.tt(c3(A), m.unsqueeze(2).to_broadcast([128, 8, 64]), c3(E), ALU.subtract, [Ek], [Ak])
                            self.tt(sm[:], Lc, m, ALU.subtract, [cumk, Ek], [smk])
                            self.act(scd(0), sm[:], AF.Exp, [smk], [sck])
                            self.act(scd(2), m, AF.Exp, [Ek], [sck])
                        self.act(scd(1), Lc, AF.Exp, [cumk], [sck])
                        eA, eAk = f32r.get()
                        self.act(eA[:], A[:], AF.Exp, [Ak], [eAk])
                        qt_, qtk_ = st[(d, "q")]
                        self.tt(qt_[:, h, :], qs[:], eA[:], ALU.mult, [qsk, eAk], [qtk_])
                        eN, eNk = f32r.get()
                        self.act(eN[:], A[:], AF.Exp, [Ak], [eNk], scale=-1.0)
                        kt_, ktk_ = st[(d, "k")]
                        self.tt(kt_[:, h, :], kk[:], eN[:], ALU.mult, [kkk, eNk], [ktk_])
                    ps, bk = proj(hT, hTk, 2048 + h * 128)
                    self.act(sg[:, h, :], ps, AF.Silu, [bk], [sgk])
                for k in self.QK:
                    tl, tk = st[k]
                    self.dma(self.QK[k].rearrange("(h p) n -> p h n", p=128)[:, :, c0:c0 + 512], tl[:], [tk], [])
                self.dma(self.SG.rearrange("(h p) n -> p h n", p=128)[:, :, c0:c0 + 512], sg[:], [sgk], [])
                qt, qtk = qtr.get()
                kt, ktk = ktr.get()
                def qk_thread(oc):
                    ps, bk = proj(hT, hTk, 2560 + oc * 128)
                    gcol = self.ppc(lb + 1) if oc < 8 else self.ppc(lb + 2)
                    sq, sqk = b16r.get()
                    self.act(sq[:], ps, AF.Square, [bk], [sqk])
                    yield
                    b2, bk2 = self.bank()
                    self.mm(self.psb(b2), self.BDb[:], sq[:], True, True, ["cst", sqk], [bk2])
                    r1, r1k = f32r.get()
                    self.act(r1[:], self.psb(b2), AF.Ln, [bk2], [r1k], bias=self.epsc[:, 0:1], scale=1.0 / 64)
                    self.act(r1[:], r1[:], AF.Exp, [r1k], [r1k], scale=-0.5)
                    zn, znk = b16r.get()
                    self.stt(zn[:], ps, gcol, r1[:], ALU.mult, ALU.mult, [bk, "pp", r1k], [znk])
                    yield
                    b3, bk3 = self.bank()
                    self.mm(self.psb(b3), self.ROTb[:], zn[:], True, True, ["cst", znk], [bk3])
                    t2, t2k = f32r.get()
                    self.tt(t2[:], zn[:], cs[:, 0, :], ALU.mult, [znk, csk], [t2k])
                    t3, t3k = f32r.get()
                    self.tt(t3[:], self.psb(b3), cs[:, 1, :], ALU.mult, [bk3, csk], [t3k])
                    if oc < 8:
                        self.tt(qt[:, oc, :], t2[:], t3[:], ALU.add, [t2k, t3k], [qtk], eng="pool")
                    else:
                        self.tt(kt[:, oc - 8, :], t2[:], t3[:], ALU.add, [t2k, t3k], [ktk], eng="pool")
                    yield

                run_pipe(qk_thread(oc) for oc in range(12))
                self.dma(self.QT.rearrange("(k p) n -> p k n", p=128)[:, :, c0:c0 + 512], qt[:], [qtk], [])
                self.dma(self.KT2.rearrange("(k p) n -> p k n", p=128)[:, :, c0:c0 + 512], kt[:], [ktk], [])
                vh, vhk = vhr.get()
                va, vak = var.get()
                for s in range(4):
                    g = t * 4 + s
                    b, bk = self.bank()
                    for k in range(8):
                        self.mm(self.psb(b), hT[:, k, s * 128:(s + 1) * 128], WA[:, k, 4096:4608], k == 0, k == 7, ["WA", hTk], [bk])
                    self.cp(vh[:, s, :], self.psb(b), [bk], [vhk], eng="act")
                    b, bk = self.bank()
                    for k in range(8):
                        self.mm(self.psb(b, 256), hT[:, k, s * 128:(s + 1) * 128], WA[:, k, 4608:4864], k == 0, k == 7, ["WA", hTk], [bk])
                    vav = va[:, s, :].rearrange("p (h c) -> p h c", c=128)
                    self.cp(vav[:, :, 0:64], self.psb(b, 256).rearrange("p (h c) -> p h c", c=64), [bk], [vak])
                    self.cp(vav[:, :, 64:128], self.maskt[:, g:g + 1].unsqueeze(1).to_broadcast([128, 4, 64]), ["mask"], [vak])
                self.dma(self.VH.rearrange("(s p) f -> p s f", p=128)[:, t * 4:(t + 1) * 4, :], vh[:], [vhk], [])
                self.dma(self.VA.rearrange("(s p) f -> p s f", p=128)[:, t * 4:(t + 1) * 4, :], va[:], [vak], [])
            self.p.barrier()

    def phase_B(self, L):
        NT = self.NT
        with ExitStack() as es:
            Sf = self.sb(es, "b_Sf", [128, 2, 4, 128], F32)
            Sb = self.sb(es, "b_Sb", [128, 2, 4, 128], BF16)
            self.ms(Sf[:], 0.0, ["Sf00", "Sf01", "Sf02", "Sf03", "Sf10", "Sf11", "Sf12", "Sf13"])
            self.ms(Sb[:], 0.0, ["Sb00", "Sb01", "Sb02", "Sb03", "Sb10", "Sb11", "Sb12", "Sb13"])
            qr = {d: self.rot(es, "b_q" + d, [128, 4, 512], BF16, 2) for d in "fb"}
            kr = {d: self.rot(es, "b_k" + d, [128, 4, 512], BF16, 2) for d in "fb"}
            vr = {d: self.rot(es, "b_v" + d, [128, 4, 512], BF16, 2) for d in "fb"}
            ktr = {d: self.rot(es, "b_kt" + d, [128, 4, 4, 128], BF16, 2) for d in "fb"}
            otr = {d: self.rot(es, "b_o" + d, [128, 4, 512], F32, 2) for d in "fb"}
            smr = self.rot(es, "b_sm", [128, 128], BF16, 8)
            tmr = self.rot(es, "b_tm", [128, 128], F32, 8)
            MK = {"f": self.CST[:, 512:640], "b": self.CST[:, 640:768]}
            for step in range(NT):
                tiles = {"f": step, "b": NT - 1 - step}
                cur = {}
                for di, d in enumerate("fb"):
                    t = tiles[d]
                    c0 = t * 512
                    q, qk = qr[d].get()
                    k, kk = kr[d].get()
                    v, vk = vr[d].get()
                    self.dma(q[:], self.QK[(d, "q")].rearrange("(h p) n -> p h n", p=128)[:, :, c0:c0 + 512], (), [qk])
                    self.dma(k[:], self.QK[(d, "k")].rearrange("(h p) n -> p h n", p=128)[:, :, c0:c0 + 512], (), [kk])
                    self.dma(v[:], self.VH.rearrange("(s p) f -> p s f", p=128)[:, t * 4:(t + 1) * 4, :], (), [vk])
                    ktk_t, ktk = ktr[d].get()
                    for h in range(4):
                        b, bk = self.bank(4, 8)
                        psv = self.psb(b).bitcast(BF16)
                        for bl in range(4):
                            self.tr(psv[:, bl * 128:(bl + 1) * 128], k[:, h, bl * 128:(bl + 1) * 128], self.identb[:], [kk, "cst"], [bk])
                        self.cp(ktk_t[:, h, :, :], psv[:, 0:512].rearrange("p (b n) -> p b n", b=4), [bk], [ktk], eng="act")
                    o, ok = otr[d].get()
                    cur[d] = (t, q, qk, k, kk, v, vk, ktk_t, ktk, o, ok)
                for bi in range(4):
                    for di, d in enumerate("fb"):
                        t, q, qk, k, kk, v, vk, ktk_t, ktk, o, ok = cur[d]
                        blk = bi if d == "f" else 3 - bi
                        order = (0, 1) if d == "f" else (1, 0)
                        bc = slice(blk * 128, (blk + 1) * 128)
                        sms = []
                        for h in range(4):
                            b, bk = self.bank(4, 8)
                            ps = self.psb(b, 128)
                            self.mm(ps, k[:, h, bc], q[:, h, bc], True, True, [kk, qk], [bk])
                            sm, smk = smr.get()
                            self.tt(sm[:], ps, MK[d], ALU.mult, [bk, "cstf"], [smk])
                            sms.append((sm, smk))

                        def upd(h, ca):
                            gch = t * 8 + blk * 2 + ca
                            sfk, sbk = "Sf%d%d" % (di, h), "Sb%d%d" % (di, h)
                            b2, bk2 = self.bank(4, 8)
                            rows = slice(ca * 64, (ca + 1) * 64)
                            self.mm(self.psb(b2, 128), ktk_t[rows, h, blk, :], v[rows, blk, h * 128:(h + 1) * 128], True, True, [ktk, vk], [bk2])
                            tm, tmk = tmr.get()
                            self.ts(tm[:], Sf[:, di, h, :], self.SC[:, di, 1, h, gch:gch + 1], None, ALU.mult, ALU.bypass, [sfk, "SC"], [tmk])
                            self.stt(Sf[:, di, h, :], self.psb(b2, 128), self.SC[:, di, 2, h, gch:gch + 1], tm[:], ALU.mult, ALU.add, [bk2, "SC", tmk], [sfk])
                            nxt = gch + 1 if d == "f" else gch - 1
                            if 0 <= nxt < self.NCH:
                                self.ts(Sb[:, di, h, :], Sf[:, di, h, :], self.SC[:, di, 0, h, nxt:nxt + 1], None, ALU.mult, ALU.bypass, [sfk, "SC"], [sbk])

                        for h in range(4):
                            sm, smk = sms[h]
                            okey = "ps%d" % h
                            O = self.psb(h, 128)
                            ca = order[0]
                            self.mm(O, v[:, blk, h * 128:(h + 1) * 128], sm[:], True, False, [vk, smk], [okey])
                            self.mm(O[:, ca * 64:(ca + 1) * 64], Sb[:, di, h, :], q[:, h, blk * 128 + ca * 64:blk * 128 + (ca + 1) * 64],
                                    False, False, ["Sb%d%d" % (di, h), qk], [okey])
                            upd(h, ca)
                        for h in range(4):
                            okey = "ps%d" % h
                            O = self.psb(h, 128)
                            cb = order[1]
                            self.mm(O[:, cb * 64:(cb + 1) * 64], Sb[:, di, h, :], q[:, h, blk * 128 + cb * 64:blk * 128 + (cb + 1) * 64],
                                    False, True, ["Sb%d%d" % (di, h), qk], [okey])
                            upd(h, cb)
                            self.cp(o[:, h, bc], O, [okey], [ok], eng="act")
                for d in "fb":
                    t, q, qk, k, kk, v, vk, ktk_t, ktk, o, ok = cur[d]
                    self.dma(self.OO[d].rearrange("(h p) n -> p h n", p=128)[:, :, t * 512:(t + 1) * 512], o[:], [ok], [])
            self.p.barrier()

    def phase_C1(self, L):
        NS, NT, NG = self.NS, self.NT, self.NG
        with ExitStack() as es:
            KT = self.sb(es, "c_KT", [128, 4, NS], BF16)
            for k in range(4):
                self.dma(KT[:, k, :], self.KT2.rearrange("(k p) n -> p k n", p=128)[:, k, :], (), ["KT"])
            VAt = self.sb(es, "c_VA", [128, NG, 512], BF16)
            VAv = self.VA.rearrange("(g p) f -> p g f", p=128)
            for g0 in range(0, NG, 16):
                g1 = min(NG, g0 + 16)
                self.dma(VAt[:, g0:g1, :], VAv[:, g0:g1, :], (), ["VAt"])
            qr = self.rot(es, "c_q", [128, 8, 512], BF16, 2)
            ptr = self.rot(es, "c_pt", [128, 1024], BF16, 3)
            yar = self.rot(es, "c_ya", [64, 16, 512], BF16, 2)
            rdr = self.rot(es, "c_rd", [128, 512], F32, 2)
            QTv = self.QT.rearrange("(k p) n -> p k n", p=128)
            YAv = self.YA.rearrange("(h d) n -> d h n", d=64)
            qtiles = {}

            def load_q(qt):
                q, qk = qr.get()
                self.dma(q[:], QTv[:, :, qt * 512:(qt + 1) * 512], (), [qk])
                qtiles[qt] = (q, qk)

            load_q(0)
            steps = [(qt, c, kt) for qt in range(NT) for c in range(8) for kt in range(NG)]
            pend = None
            ya = yak = None
            for i in range(len(steps) + 1):
                if i < len(steps):
                    qt, c, kt = steps[i]
                    if c == 0 and kt == 0:
                        if qt + 1 < NT:
                            load_q(qt + 1)
                        ya, yak = yar.get()
                    q, qk = qtiles[qt]
                    kvh = c // 2
                    s0 = (i % 2) * 2
                    kc = slice(kt * 128, (kt + 1) * 128)
                    self.mm(self.psb(s0), KT[0:64, kvh, kc], q[0:64, c, :], True, True, ["KT", qk], ["ps%d" % s0])
                    self.mm(self.psb(s0 + 1), KT[64:128, kvh, kc], q[64:128, c, :], True, True, ["KT", qk], ["ps%d" % (s0 + 1)])
                    pt, ptk = ptr.get()
                    self.act(pt[:], self.PS[:, s0 * 512:s0 * 512 + 1024], AF.Exp, ["ps%d" % s0, "ps%d" % (s0 + 1)], [ptk])
                    cur = (qt, c, kt, pt, ptk, ya, yak)
                if pend is not None:
                    pqt, pc, pkt, ppt, pptk, pya, pyak = pend
                    pkvh = pc // 2
                    ob0 = 4 + (pc % 2) * 2
                    lhs = VAt[:, pkt, pkvh * 128:(pkvh + 1) * 128]
                    for j in range(2):
                        self.mm(self.psb(ob0 + j), lhs, ppt[:, j * 512:(j + 1) * 512], pkt == 0, pkt == NG - 1,
                                ["VAt", pptk], ["ps%d" % (ob0 + j)])
                    if pkt == NG - 1:
                        for j in range(2):
                            h = 2 * pc + j
                            ob = ob0 + j
                            okey = "ps%d" % ob
                            rd, rdk = rdr.get()
                            self.p.op("dve", lambda e, rd=rd, ob=ob: e.reciprocal(rd[64:128, :], self.PS[64:128, ob * 512:(ob + 1) * 512]), [okey], [rdk])
                            self.tt(pya[:, h, :], self.PS[0:64, ob * 512:(ob + 1) * 512], rd[64:128, :], ALU.mult, [okey, rdk], [pyak])
                        if pc == 7:
                            self.dma(YAv[:, :, pqt * 512:(pqt + 1) * 512], pya[:], [pyak], [])
                pend = cur if i < len(steps) else None
            self.p.barrier()

    def phase_C2a(self, L):
        NT = self.NT
        lb = self.lbase(L)
        with ExitStack() as es:
            WG = self.sb(es, "e_WG", [128, 8, 3072], BF16)
            self.wload(WG, self.w_inG[L].rearrange("(k p) f -> p k f", p=128), "WG", 8)
            WP = self.sb(es, "e_WP", [128, 4, 128], BF16)
            self.dma(WP[:], self.poolw[L], (), ["WP"], q="pool")
            WBP = self.sb(es, "e_WBP", [128, 4, D], BF16)
            self.wload(WBP, self.wbp[L].rearrange("(k p) f -> p k f", p=128), "WBP", 4)
            WBH = self.sb(es, "e_WBH", [128, 4, D], BF16)
            self.wload(WBH, self.wbh[L].rearrange("(k p) f -> p k f", p=128), "WBH", 4)
            WBA = self.sb(es, "e_WBA", [128, 8, D], BF16)
            self.wload(WBA, self.wba[L].rearrange("(k p) f -> p k f", p=128), "WBA", 8)
            hTr = self.rot(es, "e_hT", [128, 8, 512], BF16, 1)
            ur = self.rot(es, "e_u", [128, 4, 528], BF16, 1)
            oor = self.rot(es, "e_oo", [128, 512], F32, 5)
            sgr = self.rot(es, "e_sg", [128, 4, 512], BF16, 1)
            yar = self.rot(es, "e_ya", [128, 8, 512], BF16, 1)
            rcr = self.rot(es, "e_rc", [128, 4, 512], F32, 1)
            ypr = self.rot(es, "e_yp", [128, 4, 512], BF16, 1)
            yhr = self.rot(es, "e_yh", [128, 4, 512], BF16, 1)
            mgr = self.rot(es, "e_mg", [128, 8, 512], BF16, 1)
            gtr = self.rot(es, "e_gt", [128, 512], F32, 10)
            f32r = self.rot(es, "e_f", [128, 528], F32, 9)
            b16r = self.rot(es, "e_b", [128, 512], BF16, 4)
            for t in range(NT):
                c0 = t * 512
                hT, hTk = hTr.get()
                self.dma(hT[:], self.H1T.rearrange("(k p) n -> p k n", p=128)[:, :, c0:c0 + 512], (), [hTk])
                u, uk = ur.get()
                self.dma(u[:], self.UT.rearrange("(k p) n -> p k n", p=128)[:, :, c0:c0 + 528], (), [uk])
                sg, sgk = sgr.get()
                self.dma(sg[:], self.SG.rearrange("(h p) n -> p h n", p=128)[:, :, c0:c0 + 512], (), [sgk])
                ya, yak = yar.get()
                self.dma(ya[:], self.YA.rearrange("(k p) n -> p k n", p=128)[:, :, c0:c0 + 512], (), [yak])
                rct, rck = rcr.get()
                for g in range(4):
                    self.dma(rct[:, g, :], self.rc[g:g + 1, c0:c0 + 512].partition_broadcast(128), (), [rck])
                yp, ypk = ypr.get()
                yh, yhk = yhr.get()
                mg, mgk = mgr.get()

                def pool_thread(g):
                    S, Sk = f32r.get()
                    self.tt(S[:, 1:528], u[:, g, 0:527], u[:, g, 1:528], ALU.add, [uk], [Sk], eng="pool")
                    lo, hi = 1, 528
                    for lv in range(g):
                        sh = 1 << lv
                        S2, S2k = f32r.get()
                        nlo, nhi = lo + sh, hi - sh
                        self.tt(S2[:, nlo:nhi], S[:, nlo - sh:nhi - sh], S[:, nlo + sh:nhi + sh], ALU.add, [Sk], [S2k], eng="pool")
                        S, Sk, lo, hi = S2, S2k, nlo, nhi
                    yield
                    m1, m1k = f32r.get()
                    self.tt(m1[:, 0:512], S[:, 8:520], rct[:, g, :], ALU.mult, [Sk, rck], [m1k])
                    mx, mxk = b16r.get()
                    self.tt(mx[:], m1[:, 0:512], u[:, g, 8:520], ALU.subtract, [m1k, uk], [mxk])
                    yield
                    b, bk = self.bank()
                    self.mm(self.psb(b), WP[:, g, :], mx[:], True, True, ["WP", mxk], [bk])
                    self.act(yp[:, g, :], self.psb(b), AF.Identity, [bk, "pp"], [ypk], scale=self.ppc(lb + 3 + g), bias=0.0)
                    yield

                def hg_thread(h):
                    of, ofk = oor.get()
                    ob, obk = f32r.get()
                    self.dma(of[:], self.OO["f"][h * 128:(h + 1) * 128, c0:c0 + 512], (), [ofk])
                    self.dma(ob[:, 0:512], self.OO["b"][h * 128:(h + 1) * 128, c0:c0 + 512], (), [obk])
                    self.tt(of[:], of[:], ob[:, 0:512], ALU.add, [ofk, obk], [ofk])
                    yield
                    sq, sqk = b16r.get()
                    self.act(sq[:], of[:], AF.Square, [ofk], [sqk])
                    yield
                    b, bk = self.bank()
                    self.mm(self.psb(b), self.ONESb[:], sq[:], True, True, ["cst", sqk], [bk])
                    r1, r1k = f32r.get()
                    self.act(r1[:, 0:512], self.psb(b), AF.Ln, [bk], [r1k], bias=self.epsc[:, 0:1], scale=1.0 / 128)
                    self.act(r1[:, 0:512], r1[:, 0:512], AF.Exp, [r1k], [r1k], scale=-0.5)
                    yield
                    self.stt(of[:], of[:], self.ppc(lb), r1[:, 0:512], ALU.mult, ALU.mult, [ofk, "pp", r1k], [ofk])
                    self.tt(yh[:, h, :], of[:], sg[:, h, :], ALU.mult, [ofk, sgk], [yhk])
                    yield

                def oc_thread(oc):
                    ocs = slice(oc * 128, (oc + 1) * 128)
                    gts = []
                    for j in range(3):
                        bg, bgk = self.bank()
                        for k in range(8):
                            self.mm(self.psb(bg), WG[:, k, j * D + oc * 128:j * D + (oc + 1) * 128], hT[:, k, :], k == 0, k == 7, ["WG", hTk], [bgk])
                        gt, gtk = gtr.get()
                        self.act(gt[:], self.psb(bg), AF.Sigmoid, [bgk], [gtk])
                        gts.append((gt, gtk))
                    yield
                    yield
                    for j, (W_, wk, src, srck, nk) in enumerate(((WBP, "WBP", yp, ypk, 4), (WBH, "WBH", yh, yhk, 4), (WBA, "WBA", ya, yak, 8))):
                        bb, bbk = self.bank()
                        for k in range(nk):
                            self.mm(self.psb(bb), W_[:, k, ocs], src[:, k, :], k == 0, k == nk - 1, [wk, srck], [bbk])
                        gt, gtk = gts[j]
                        self.tt(gt[:], gt[:], self.psb(bb), ALU.mult, [gtk, bbk], [gtk])
                    self.tt(gts[0][0][:], gts[0][0][:], gts[1][0][:], ALU.add, [gts[0][1], gts[1][1]], [gts[0][1]], eng="pool")
                    self.tt(mg[:, oc, :], gts[0][0][:], gts[2][0][:], ALU.add, [gts[0][1], gts[2][1]], [mgk], eng="pool")
                    yield

                run_pipe([hg_thread(h) for h in range(4)] + [pool_thread(g) for g in range(4)] + [oc_thread(oc) for oc in range(8)])
                self.dma(self.MG.rearrange("(k p) n -> p k n", p=128)[:, :, c0:c0 + 512], mg[:], [mgk], [])
            self.p.barrier()

    def phase_C2b(self, L):
        NT = self.NT
        with ExitStack() as es:
            WO = self.sb(es, "f_WO", [128, 8, D], BF16)
            self.wload(WO, self.wout[L].rearrange("(k p) f -> p k f", p=128), "WO", 8)
            gf = self.sb(es, "f_g", [128, D], F32)
            self.load_g(gf, "gf", 3 * L + 1)
            mgr = self.rot(es, "f_mg", [128, 8, 512], BF16, 2)
            xr = self.rot(es, "f_x", [128, D], F32, 4)
            hr = self.rot(es, "f_h", [128, D], BF16, 3)
            hTr = self.rot(es, "f_hT", [128, 8, 512], BF16, 2)
            tiles = {}

            def grp(g):
                t, s_ = g // 4, g % 4
                c0 = t * 512
                if s_ == 0:
                    mg, mgk = mgr.get()
                    self.dma(mg[:], self.MG.rearrange("(k p) n -> p k n", p=128)[:, :, c0:c0 + 512], (), [mgk])
                    tiles[t] = (mg, mgk) + hTr.get()
                mg, mgk, hT, hTk = tiles[t]
                x, xk = xr.get()
                self.dma(x[:], self.XS[g * 128:(g + 1) * 128, :], (), [xk])
                for fh in range(2):
                    b, bk = self.bank()
                    for k in range(8):
                        self.mm(self.psb(b), mg[:, k, s_ * 128:(s_ + 1) * 128], WO[:, k, fh * 512:(fh + 1) * 512], k == 0, k == 7, [mgk, "WO"], [bk])
                    self.tt(x[:, fh * 512:(fh + 1) * 512], x[:, fh * 512:(fh + 1) * 512], self.psb(b), ALU.add, [xk, bk], [xk])
                self.dma(self.XS[g * 128:(g + 1) * 128, :], x[:], [xk], [])
                h, hk = hr.get()
                self.norm_rows(x[:], xk, gf[:], "gf", self.maskt[:, g:g + 1], h[:], hk)
                yield
                self.transpose_rows(h[:], hk, hT, hTk, s_ * 128)
                if s_ == 3:
                    self.dma(self.H2T.rearrange("(k p) n -> p k n", p=128)[:, :, 1 + c0:1 + c0 + 512], hT[:], [hTk], [])
                yield

            run_pipe(grp(g) for g in range(self.NG))
            self.p.barrier()

    def phase_D1(self, L):
        NS = self.NS
        lb = self.lbase(L)
        cw = lb + 7
        with ExitStack() as es:
            WU = self.sb(es, "g_WU", [128, 8, 2 * DFF], BF16)
            self.wload(WU, self.wup[L].rearrange("(k p) f -> p k f", p=128), "WU", 8)
            WD = self.sb(es, "g_WD", [128, 22, D], BF16)
            self.wload(WD, self.wdn[L].rearrange("(k p) f -> p k f", p=128), "WD", 22)
            hTr = self.rot(es, "g_hT", [128, 8, 384], BF16, 2)
            acr = self.rot(es, "g_ac", [128, 22, 382], BF16, 2)
            xr = self.rot(es, "g_x", [128, D], F32, 1)
            f32r = self.rot(es, "g_f", [128, 382], F32, 9)
            H2v = self.H2T.rearrange("(k p) n -> p k n", p=128)

            def conv(ps, bk, ci, w):
                t1, t1k = f32r.get()
                self.act(t1[:, 0:w], ps[:, 1:w + 1], AF.Identity, [bk, "pp"], [t1k], scale=self.ppc(cw + NFC + ci), bias=self.ppc(cw + 3 * NFC + ci))
                self.stt(t1[:, 0:w], ps[:, 0:w], self.ppc(cw + ci), t1[:, 0:w], ALU.mult, ALU.add, [bk, "pp", t1k], [t1k])
                self.stt(t1[:, 0:w], ps[:, 2:w + 2], self.ppc(cw + 2 * NFC + ci), t1[:, 0:w], ALU.mult, ALU.add, [bk, "pp", t1k], [t1k])
                return t1, t1k

            def tile(c0, w):
                hT, hTk = hTr.get()
                self.dma(hT[:, :, 0:w + 2], H2v[:, :, c0:c0 + w + 2], (), [hTk])
                ac, ack = acr.get()

                def pair(j):
                    res = []
                    for ci in (j, 22 + j):
                        b, bk = self.bank()
                        ps = self.psb(b, w + 2)
                        for k in range(8):
                            self.mm(ps, WU[:, k, ci * 128:(ci + 1) * 128], hT[:, k, 0:w + 2], k == 0, k == 7, ["WU", hTk], [bk])
                        t1, t1k = f32r.get()
                        self.act(t1[:, 0:w], ps[:, 1:w + 1], AF.Identity, [bk, "pp"], [t1k], scale=self.ppc(cw + NFC + ci), bias=self.ppc(cw + 3 * NFC + ci))
                        res.append((ps, bk, ci, t1, t1k))
                    yield
                    for ps, bk, ci, t1, t1k in res:
                        self.stt(t1[:, 0:w], ps[:, 0:w], self.ppc(cw + ci), t1[:, 0:w], ALU.mult, ALU.add, [bk, "pp", t1k], [t1k])
                        self.stt(t1[:, 0:w], ps[:, 2:w + 2], self.ppc(cw + 2 * NFC + ci), t1[:, 0:w], ALU.mult, ALU.add, [bk, "pp", t1k], [t1k])
                    yield
                    (_, _, _, a, ak), (_, _, _, gg, ggk) = res
                    self.act(a[:, 0:w], a[:, 0:w], AF.Gelu_apprx_tanh, [ak], [ak])
                    yield
                    self.tt(ac[:, j, 0:w], a[:, 0:w], gg[:, 0:w], ALU.mult, [ak, ggk], [ack])
                    yield

                run_pipe(pair(j) for j in range(22))
                yield
                for s0 in range(0, w, 128):
                    m = min(128, w - s0)
                    r0 = c0 + s0
                    x, xk = xr.get()
                    self.dma(x[0:m, :], self.XS[r0:r0 + m, :], (), [xk])
                    for fh in range(2):
                        b, bk = self.bank()
                        for k in range(22):
                            self.mm(self.PS[0:m, b * 512:(b + 1) * 512], ac[:, k, s0:s0 + m], WD[:, k, fh * 512:(fh + 1) * 512], k == 0, k == 21, [ack, "WD"], [bk])
                        self.tt(x[0:m, fh * 512:(fh + 1) * 512], x[0:m, fh * 512:(fh + 1) * 512], self.PS[0:m, b * 512:(b + 1) * 512], ALU.add, [xk, bk], [xk])
                    self.dma(self.XS[r0:r0 + m, :], x[0:m, :], [xk], [])
                yield

            starts = list(range(0, NS, 382))
            run_pipe(tile(c0, min(382, NS - c0)) for c0 in starts)
            self.p.barrier()

    def phase_D2(self, L):
        NT = self.NT
        last = L == self.DEPTH - 1
        with ExitStack() as es:
            WPG = self.sb(es, "h_WPG", [128, 8, D], BF16)
            self.wload(WPG, self.wpg[L].rearrange("(k p) f -> p k f", p=128), "WPG", 8)
            WPL = self.sb(es, "h_WPL", [128, 2, D], BF16)
            self.wload(WPL, self.wpl[L].rearrange("(k p) f -> p k f", p=128), "WPL", 2)
            gp = self.sb(es, "h_gp", [128, D], F32)
            self.load_g(gp, "gp", 3 * L + 2)
            gn = self.sb(es, "h_gn", [128, D], F32)
            self.load_g(gn, "gn", 3 * self.DEPTH if last else 3 * (L + 1))
            ptr = self.rot(es, "h_pT", [128, 2, 512], BF16, 2)
            xr = self.rot(es, "h_x", [128, D], F32, 5)
            hr = self.rot(es, "h_h", [128, D], BF16, 6)
            h3r = self.rot(es, "h_h3T", [128, 8, 128], BF16, 3)
            hTr = self.rot(es, "h_hT", [128, 8, 512], BF16, 2)
            yr = self.rot(es, "h_y", [128, D], F32, 2)
            gsr = self.rot(es, "h_gs", [128, 512], F32, 4)
            tiles = {}

            def grp(g):
                t, s_ = g // 4, g % 4
                c0 = t * 512
                if s_ == 0:
                    pt, ptk = ptr.get()
                    self.dma(pt[:], self.pT[L].rearrange("(k p) n -> p k n", p=128)[:, :, c0:c0 + 512], (), [ptk], q="pool")
                    tiles[t] = (pt, ptk) + hTr.get()
                pt, ptk, hT, hTk = tiles[t]
                x, xk = xr.get()
                self.dma(x[:], self.XS[g * 128:(g + 1) * 128, :], (), [xk])
                h, hk = hr.get()
                self.norm_rows(x[:], xk, gp[:], "gp", None, h[:], hk)
                yield
                h3, h3k = h3r.get()
                self.transpose_rows(h[:], hk, h3, h3k, 0)
                yield
                for fh in range(2):
                    fs = slice(fh * 512, (fh + 1) * 512)
                    b, bk = self.bank()
                    for k in range(8):
                        self.mm(self.psb(b), h3[:, k, :], WPG[:, k, fs], k == 0, k == 7, [h3k, "WPG"], [bk])
                    gs, gsk = gsr.get()
                    self.act(gs[:], self.psb(b), AF.Sigmoid, [bk], [gsk])
                    b2, bk2 = self.bank()
                    for k in range(2):
                        self.mm(self.psb(b2), pt[:, k, s_ * 128:(s_ + 1) * 128], WPL[:, k, fs], k == 0, k == 1, [ptk, "WPL"], [bk2])
                    self.tt(gs[:], gs[:], self.psb(b2), ALU.mult, [gsk, bk2], [gsk])
                    self.tt(x[:, fs], x[:, fs], gs[:], ALU.add, [xk, gsk], [xk])
                if last:
                    yt, ytk = yr.get()
                    self.norm_rows(x[:], xk, gn[:], "gn", None, yt[:], ytk)
                    self.dma(self.y[g * 128:(g + 1) * 128, :], yt[:], [ytk], [])
                    yield
                else:
                    self.dma(self.XS[g * 128:(g + 1) * 128, :], x[:], [xk], [])
                    h2, h2k = hr.get()
                    self.norm_rows(x[:], xk, gn[:], "gn", self.maskt[:, g:g + 1], h2[:], h2k)
                    yield
                    self.transpose_rows(h2[:], h2k, hT, hTk, s_ * 128)
                    if s_ == 3:
                        self.dma(self.H1T.rearrange("(k p) n -> p k n", p=128)[:, :, c0:c0 + 512], hT[:], [hTk], [])
                    yield

            run_pipe(grp(g) for g in range(self.NG))
            self.p.barrier()


def _consts(NS, n_real):
    cst = np.zeros((128, 1280), np.float32)
    idx = np.arange(128)
    cst[:, 0:128] = np.eye(128, dtype=np.float32)
    cst[:, 128:256] = (idx[:, None] // 64 == idx[None, :] // 64)
    cst[:, 256:384] = 1.0
    rot = np.zeros((128, 128), np.float32)
    for m in range(128):
        if (m % 32) < 16:
            rot[m + 16, m] = -1.0
        else:
            rot[m - 16, m] = 1.0
    cst[:, 384:512] = rot
    same = idx[:, None] // 64 == idx[None, :] // 64
    cst[:, 512:640] = same & (idx[:, None] <= idx[None, :])
    cst[:, 640:768] = same & (idx[:, None] >= idx[None, :])
    cst[:, 768:1280] = (np.arange(512) % 64 != 0)[None, :]
    t = np.arange(NS)
    maskc = (t < n_real).astype(np.float32).reshape(NS // 128, 128).T.copy()
    rc = np.ones((4, NS), np.float32)
    for g, w in enumerate((2, 4, 8, 16)):
        lo = np.clip(t - w // 2, 0, n_real)
        hi = np.clip(t + w // 2, 0, n_real)
        c = np.maximum(hi - lo, 1).astype(np.float32)
        rc[g] = np.where(t < n_real, np.float32(1.0) / c, np.float32(1.0))
    inv_freq = (1.0 / (np.float32(10000.0) ** (np.arange(0, 32, 2, dtype=np.float32) / np.float32(32)))).astype(np.float32)
    d = idx % 64
    a = d // 32
    i = d % 16
    pos = np.where(a[:, None] == 0, (t // 64)[None, :], (t % 64)[None, :]).astype(np.float32)
    ang = pos * inv_freq[i][:, None]
    return cst, maskc, rc, np.cos(ang).astype(np.float32), np.sin(ang).astype(np.float32)


def _shared(inp, DEPTH):
    f = lambda k: np.asarray(inp[k], np.float32)
    w_in = f("w_in")
    cols = list(range(0, 2048)) + list(range(2560, 3072)) + list(range(3072, 4096))
    for kvh in range(4):
        cols += list(range(4096 + kvh * 64, 4096 + (kvh + 1) * 64)) * 2
    cols += list(range(2048, 2560)) + list(range(4352, 4608))
    sh = {}
    sh["w_inA"] = np.ascontiguousarray(w_in[:, :, cols])
    sh["w_inG"] = np.ascontiguousarray(w_in[:, :, 4608:7680])
    sh["poolw"] = np.ascontiguousarray(f("pool_w").transpose(0, 2, 1, 3))
    sh["wbp"], sh["wbh"], sh["wba"] = f("w_br_pool"), f("w_br_hg"), f("w_br_att")
    sh["wout"], sh["wup"], sh["wdn"] = f("w_out"), f("w_up"), f("w_down")
    sh["wpg"], sh["wpl"] = f("w_ple_gate"), f("w_ple")
    gv = np.zeros((3 * DEPTH + 1, D), np.float32)
    for L in range(DEPTH):
        gv[3 * L], gv[3 * L + 1], gv[3 * L + 2] = f("g_mix")[L], f("g_ffn")[L], f("g_ple")[L]
    gv[3 * DEPTH] = f("g_final")
    sh["gv"] = gv
    pp = np.zeros((128, 8 * DEPTH + DEPTH * PL), np.float32)
    pp[:, 0:4 * DEPTH] = f("lb_raw_f").reshape(DEPTH, 4, 128).transpose(2, 0, 1).reshape(128, 4 * DEPTH)
    pp[:, 4 * DEPTH:8 * DEPTH] = f("lb_raw_b").reshape(DEPTH, 4, 128).transpose(2, 0, 1).reshape(128, 4 * DEPTH)
    p64 = np.arange(128) % 64
    for L in range(DEPTH):
        b = 8 * DEPTH + L * PL
        pp[:, b] = f("hg_onorm")[L]
        pp[:, b + 1] = f("g_q")[L][p64]
        pp[:, b + 2] = f("g_k")[L][p64]
        pp[:, b + 3:b + 7] = f("pool_scale")[L].reshape(4, 128).T
        cw = f("conv_w")[L].reshape(3, NFC, 128)
        pp[:, b + 7:b + 7 + 3 * NFC] = cw.transpose(2, 0, 1).reshape(128, 3 * NFC)
        pp[:, b + 7 + 3 * NFC:b + 7 + 4 * NFC] = f("conv_b")[L].reshape(NFC, 128).T
    sh["pp"] = pp
    return sh


def _core_map(sh, xs, ps, NS, DEPTH):
    n = xs.shape[0]
    x = np.zeros((NS, D), np.float32)
    x[:n] = xs
    pT = np.zeros((DEPTH, 256, NS), np.float32)
    pT[:, :, :n] = ps.transpose(0, 2, 1)
    cst, maskc, rc, cosT, sinT = _consts(NS, n)
    m = dict(sh)
    m.update(x=x, pT=pT, cst=cst, maskc=maskc, rc=rc, cosT=cosT, sinT=sinT)
    return m


_NC_CACHE = {}


def run_seqs(inp, seqs, NS, DEPTH, dbg=False):
    key = (NS, DEPTH, dbg)
    if key not in _NC_CACHE:
        _NC_CACHE[key] = Builder(NS, DEPTH, dbg).build()
    nc = _NC_CACHE[key]
    sh = _shared(inp, DEPTH)
    in_maps = [_core_map(sh, np.asarray(xs, np.float32), np.asarray(ps, np.float32), NS, DEPTH) for xs, ps in seqs]
    res = run_bass_kernel_spmd(nc, in_maps, core_ids=list(range(len(seqs))))
    return res


def kernel(**inp):
    xp, xsm = np.asarray(inp["x_prompt"]), np.asarray(inp["x_sample"])
    pp_, ps_ = np.asarray(inp["p_prompt"]), np.asarray(inp["p_sample"])
    DEPTH = pp_.shape[0]
    NS = max(xp.shape[1], xsm.shape[1])
    seqs = [(xp[i], pp_[:, i]) for i in range(xp.shape[0])] + [(xsm[i], ps_[:, i]) for i in range(xsm.shape[0])]
    res = run_seqs(inp, seqs, NS, DEPTH)
    nb = xp.shape[0]
    yp = np.stack([res.results[i]["y"][:xp.shape[1]] for i in range(nb)]).astype(np.float32)
    ys = np.stack([res.results[nb + i]["y"][:xsm.shape[1]] for i in range(xsm.shape[0])]).astype(np.float32)
    return (yp, ys)
```
